# Optimizing a Trainium2 kernel written in Bass

```python
import functools
import jax, jax.numpy as jnp
from jax import lax
import numpy as np


D_MODEL = 1024
BATCH = 8
SEQ = 2048
DEPTH = 4
DEC_BATCH = 32
DEC_SEQ = 1
PAST_LEN = 8192
PAGE_SIZE = 128

HEAD_DIM = 64
HEADS_PER_GROUP = 4
GROUPS = ((128, 1), (512, 4), (2048, 16))
N_GROUPS = len(GROUPS)
N_HEADS = HEADS_PER_GROUP * N_GROUPS
ATTN_WIDTH = N_HEADS * HEAD_DIM
CONV_WIDTH = (3 * D_MODEL) // 4
CONV_K = 31
ROT_DIM = HEAD_DIM // 4
ROPE_THETA = 500000.0
QBLK = 128
RMS_EPS = 1e-6
LN_EPS = 1e-5
IN_SIZES = (ATTN_WIDTH, ATTN_WIDTH, ATTN_WIDTH, ATTN_WIDTH,
            CONV_WIDTH, CONV_WIDTH, CONV_WIDTH, D_MODEL, D_MODEL)
IN_COLS = sum(IN_SIZES)

kernel_name = 'dilated_attn_conformer_conv_gated_hybrid_step'


def rmsnorm(x, g):
    xf = x.astype(jnp.float32)
    y = xf * lax.rsqrt(jnp.mean(xf * xf, axis=-1, keepdims=True) + RMS_EPS)
    return (y * g.astype(jnp.float32)).astype(x.dtype)


def layernorm(x, g, b):
    xf = x.astype(jnp.float32)
    mu = jnp.mean(xf, axis=-1, keepdims=True)
    xc = xf - mu
    var = jnp.mean(xc * xc, axis=-1, keepdims=True)
    y = xc * lax.rsqrt(var + LN_EPS) * g.astype(jnp.float32) + b.astype(jnp.float32)
    return y.astype(x.dtype)


def partial_rope(x, pos):
    inv = jnp.power(jnp.float32(ROPE_THETA), -jnp.arange(0, ROT_DIM, 2, dtype=jnp.float32) / ROT_DIM)
    ang = pos.astype(jnp.float32)[:, None] * inv[None, :]
    cos = jnp.cos(ang)[None, :, None, :]
    sin = jnp.sin(ang)[None, :, None, :]
    xr = x[..., :ROT_DIM].astype(jnp.float32)
    x1, x2 = xr[..., :ROT_DIM // 2], xr[..., ROT_DIM // 2:]
    rot = jnp.concatenate([x1 * cos - x2 * sin, x2 * cos + x1 * sin], axis=-1)
    return jnp.concatenate([rot.astype(x.dtype), x[..., ROT_DIM:]], axis=-1)


def group_attend(q, k_ext, v_ext, key_idx, valid):
    kg = k_ext[:, key_idx]
    vg = v_ext[:, key_idx]
    s = jnp.einsum('bqhd,bqkhd->bhqk', q, kg).astype(jnp.float32) * (HEAD_DIM ** -0.5)
    s = jnp.where(valid[None, None], s, jnp.finfo(jnp.float32).min)
    m = jnp.max(s, axis=-1, keepdims=True)
    p = jnp.exp(s - m)
    den = jnp.sum(p, axis=-1, keepdims=True)
    o = jnp.einsum('bhqk,bqkhd->bqhd', p.astype(vg.dtype), vg).astype(jnp.float32)
    o = o / jnp.transpose(den, (0, 2, 1, 3))
    lse = (m + jnp.log(den))[..., 0]
    return o, lse


def combine_groups(outs, lses, dtype):
    alpha = jax.nn.softmax(jnp.stack(lses, axis=0), axis=0)
    parts = [o * jnp.transpose(alpha[g], (0, 2, 1))[..., None] for g, o in enumerate(outs)]
    y = jnp.concatenate(parts, axis=2)
    return y.reshape(y.shape[0], y.shape[1], ATTN_WIDTH).astype(dtype)


def attn_prompt(q, k, v):
    B, S = q.shape[0], q.shape[1]
    i = jnp.arange(QBLK)
    kps, vps, states_k, states_v = [], [], [], []
    for g, (W, d) in enumerate(GROUPS):
        hs = slice(g * HEADS_PER_GROUP, (g + 1) * HEADS_PER_GROUP)
        kg, vg = k[:, :, hs], v[:, :, hs]
        kps.append(jnp.pad(kg, ((0, 0), (W, 0), (0, 0), (0, 0))))
        vps.append(jnp.pad(vg, ((0, 0), (W, 0), (0, 0), (0, 0))))
        keep = min(W, S)
        states_k.append(kg[:, S - keep:])
        states_v.append(vg[:, S - keep:])

    def block(b):
        s0 = b * QBLK
        qb = lax.dynamic_slice_in_dim(q, s0, QBLK, axis=1)
        outs, lses = [], []
        for g, (W, d) in enumerate(GROUPS):
            hs = slice(g * HEADS_PER_GROUP, (g + 1) * HEADS_PER_GROUP)
            j = jnp.arange(W // d + 1)
            kb = lax.dynamic_slice_in_dim(kps[g], s0, W + QBLK, axis=1)
            vb = lax.dynamic_slice_in_dim(vps[g], s0, W + QBLK, axis=1)
            key_idx = W + i[:, None] - d * j[None, :]
            valid = (s0 + i[:, None] - d * j[None, :]) >= 0
            o, l = group_attend(qb[:, :, hs], kb, vb, key_idx, valid)
            outs.append(o)
            lses.append(l)
        return combine_groups(outs, lses, q.dtype)

    y = lax.map(block, jnp.arange(S // QBLK))
    y = jnp.transpose(y, (1, 0, 2, 3)).reshape(B, S, ATTN_WIDTH)
    return y, (states_k, states_v)


def attn_sample(q, k, v, caches_k, caches_v):
    T = q.shape[1]
    i = jnp.arange(T)
    outs, lses, states_k, states_v = [], [], [], []
    for g, (W, d) in enumerate(GROUPS):
        hs = slice(g * HEADS_PER_GROUP, (g + 1) * HEADS_PER_GROUP)
        ck, cv = caches_k[g], caches_v[g]
        L = ck.shape[1]
        ke = jnp.concatenate([ck.astype(k.dtype), k[:, :, hs]], axis=1)
        ve = jnp.concatenate([cv.astype(v.dtype), v[:, :, hs]], axis=1)
        j = jnp.arange(W // d + 1)
        key_idx = L + i[:, None] - d * j[None, :]
        valid = key_idx >= 0
        o, l = group_attend(q[:, :, hs], ke, ve, jnp.maximum(key_idx, 0), valid)
        outs.append(o)
        lses.append(l)
        keep = min(W, L + T)
        states_k.append(ke[:, L + T - keep:])
        states_v.append(ve[:, L + T - keep:])
    return combine_groups(outs, lses, q.dtype), (states_k, states_v)


def causal_dwconv(u_ext, w):
    C = u_ext.shape[-1]
    return lax.conv_general_dilated(u_ext, w[:, None, :].astype(u_ext.dtype), window_strides=(1,),
                                    padding='VALID', dimension_numbers=('NWC', 'WIO', 'NWC'),
                                    feature_group_count=C)


def conv_prompt(u, w):
    ext = jnp.pad(u, ((0, 0), (CONV_K - 1, 0), (0, 0)))
    return causal_dwconv(ext, w), u[:, u.shape[1] - (CONV_K - 1):]


def conv_sample(u, w, buf):
    ext = jnp.concatenate([buf.astype(u.dtype), u], axis=1)
    return causal_dwconv(ext, w), ext[:, ext.shape[1] - (CONV_K - 1):]


def trunk_layer(x, pos, attn_mixer, conv_mixer, w_in, w_ao, w_co, w_o, cw, ln_g, ln_b, g_pre, g_post):
    B, S = x.shape[0], x.shape[1]
    h = rmsnorm(x, g_pre)
    z = h @ w_in
    q, k, v, ga, ca, cb, gc, ma, mc = jnp.split(z, list(np.cumsum(IN_SIZES)[:-1]), axis=-1)
    q = partial_rope(q.reshape(B, S, N_HEADS, HEAD_DIM), pos)
    k = partial_rope(k.reshape(B, S, N_HEADS, HEAD_DIM), pos)
    v = v.reshape(B, S, N_HEADS, HEAD_DIM)
    a, kv_state = attn_mixer(q, k, v)
    ya = (a * jax.nn.silu(ga)) @ w_ao
    u = ca * jax.nn.sigmoid(cb)
    cdw, conv_state = conv_mixer(u, cw)
    yc = (jax.nn.silu(layernorm(cdw, ln_g, ln_b)) * jax.nn.silu(gc)) @ w_co
    merged = jax.nn.sigmoid(ma) * ya + jax.nn.sigmoid(mc) * yc
    out = rmsnorm(merged @ w_o, g_post)
    return x + out, kv_state, conv_state


def setup_inputs(seed: int = 0) -> dict:
    key = jax.random.key(seed)
    ks = jax.random.split(key, 20)
    f32 = jnp.float32
    d = {}
    d['x_prompt'] = jax.random.normal(ks[0], (BATCH, SEQ, D_MODEL), f32)
    d['x_sample'] = jax.random.normal(ks[1], (DEC_BATCH, DEC_SEQ, D_MODEL), f32)
    for g, (W, _) in enumerate(GROUPS):
        L = min(W, PAST_LEN)
        shp = (DEPTH, DEC_BATCH, L, HEADS_PER_GROUP, HEAD_DIM)
        d['cache_k%d' % g] = jax.random.normal(ks[2 + 2 * g], shp, f32)
        d['cache_v%d' % g] = jax.random.normal(ks[3 + 2 * g], shp, f32)
    d['state_conv'] = jax.random.normal(ks[8], (DEPTH, DEC_BATCH, CONV_K - 1, CONV_WIDTH), f32)
    d['w_in'] = jax.random.normal(ks[9], (DEPTH, D_MODEL, IN_COLS), f32) * D_MODEL ** -0.5
    d['w_attn_out'] = jax.random.normal(ks[10], (DEPTH, ATTN_WIDTH, D_MODEL), f32) * ATTN_WIDTH ** -0.5
    d['w_conv_out'] = jax.random.normal(ks[11], (DEPTH, CONV_WIDTH, D_MODEL), f32) * CONV_WIDTH ** -0.5
    d['w_out'] = jax.random.normal(ks[12], (DEPTH, D_MODEL, D_MODEL), f32) * D_MODEL ** -0.5
    d['conv_w'] = jax.random.normal(ks[13], (DEPTH, CONV_K, CONV_WIDTH), f32) * CONV_K ** -0.5
    d['conv_ln_g'] = 1.0 + 0.02 * jax.random.normal(ks[14], (DEPTH, CONV_WIDTH), f32)
    d['conv_ln_b'] = 0.02 * jax.random.normal(ks[15], (DEPTH, CONV_WIDTH), f32)
    d['norm_pre'] = 1.0 + 0.02 * jax.random.normal(ks[16], (DEPTH, D_MODEL), f32)
    d['norm_post'] = 1.0 + 0.02 * jax.random.normal(ks[17], (DEPTH, D_MODEL), f32)
    return d


def reference(x_prompt, x_sample, cache_k0, cache_v0, cache_k1, cache_v1, cache_k2, cache_v2,
              state_conv, w_in, w_attn_out, w_conv_out, w_out, conv_w, conv_ln_g, conv_ln_b,
              norm_pre, norm_post):
    pos_p = jnp.arange(x_prompt.shape[1], dtype=jnp.int32)
    pos_s = PAST_LEN + jnp.arange(x_sample.shape[1], dtype=jnp.int32)
    yp, ys = x_prompt, x_sample
    pk = [[] for _ in GROUPS]
    pv = [[] for _ in GROUPS]
    sk = [[] for _ in GROUPS]
    sv = [[] for _ in GROUPS]
    pc, sc = [], []
    for l in range(DEPTH):
        params = (w_in[l], w_attn_out[l], w_conv_out[l], w_out[l], conv_w[l],
                  conv_ln_g[l], conv_ln_b[l], norm_pre[l], norm_post[l])
        yp, (kst, vst), cst = trunk_layer(yp, pos_p, attn_prompt, conv_prompt, *params)
        for g in range(N_GROUPS):
            pk[g].append(kst[g])
            pv[g].append(vst[g])
        pc.append(cst)
        s_attn = functools.partial(attn_sample,
                                   caches_k=(cache_k0[l], cache_k1[l], cache_k2[l]),
                                   caches_v=(cache_v0[l], cache_v1[l], cache_v2[l]))
        s_conv = functools.partial(conv_sample, buf=state_conv[l])
        ys, (kst, vst), cst = trunk_layer(ys, pos_s, s_attn, s_conv, *params)
        for g in range(N_GROUPS):
            sk[g].append(kst[g])
            sv[g].append(vst[g])
        sc.append(cst)
    return (yp, ys,
            jnp.stack(pk[0]), jnp.stack(pv[0]), jnp.stack(pk[1]), jnp.stack(pv[1]),
            jnp.stack(pk[2]), jnp.stack(pv[2]), jnp.stack(pc),
            jnp.stack(sk[0]), jnp.stack(sv[0]), jnp.stack(sk[1]), jnp.stack(sv[1]),
            jnp.stack(sk[2]), jnp.stack(sv[2]), jnp.stack(sc))
```

```python
import numpy as np
import concourse.bass as bass
import concourse.mybir as mybir
from concourse.alu_op_type import AluOpType as ALU
from concourse.bass_utils import run_bass_kernel_spmd
from contextlib import ExitStack

F32 = mybir.dt.float32
BF16 = mybir.dt.bfloat16
AF = mybir.ActivationFunctionType
AX = mybir.AxisListType

PE, ACT, DVE, POOL, SP = "tensor", "scalar", "vector", "gpsimd", "sync"

NL = 4
S = 2048
NS = 4
TOK = S + NS
DM = 1024
NIN = 7424
GROUPS = ((128, 1), (512, 4), (2048, 16))
RMS_EPS = 1e-6
LN_EPS = 1e-5


class Prog:
    def __init__(self, nc, es):
        self.nc = nc
        self.es = es
        self.ops = []

    def op(self, eng, emit, reads=(), writes=(), dma=None, nobar=False):
        if getattr(self, "disabled", False):
            return
        bk = [k for k in reads if k.startswith("bank")]
        if bk:
            reads = [k for k in reads if not k.startswith("bank")]
            writes = list(writes) + bk
        self.ops.append(dict(eng=eng, emit=emit, reads=tuple(reads), writes=tuple(writes), dma=dma, bar=False, nobar=nobar))

    def barrier(self):
        self.ops.append(dict(bar=True))

    def mark(self, n):
        import os
        lim = float(os.environ.get("KSTOP", "99"))
        self.disabled = n > lim

    def build(self):
        nc, es = self.nc, self.es
        ops = self.ops
        engs = (PE, ACT, DVE, POOL, SP)
        last_w = {}
        readers = {}
        n = len(ops)
        deps = [set() for _ in range(n)]
        last_on_eng = {}
        all_dma_since = set()
        bar_deps = {e: set() for e in engs}
        for i, o in enumerate(ops):
            if o["bar"]:
                snap = set(last_on_eng.values()) | all_dma_since
                for e in engs:
                    bar_deps[e] |= snap
                last_w = {k: v for k, v in last_w.items() if k.startswith("W:")}
                readers = {k: v for k, v in readers.items() if k.startswith("W:")}
                all_dma_since = set()
                continue
            d = set()
            for k in o["reads"]:
                if k in last_w:
                    d.add(last_w[k])
            for k in o["writes"]:
                if k in last_w:
                    d.add(last_w[k])
                for r in readers.get(k, ()):
                    d.add(r)
            d.discard(i)
            if o["eng"] == PE and o["dma"] is None:
                d = {j for j in d if not (ops[j]["eng"] == PE and ops[j]["dma"] is None)}
            if bar_deps[o["eng"]]:
                d |= bar_deps[o["eng"]]
                bar_deps[o["eng"]] = set()
            deps[i] = d
            for k in o["reads"]:
                readers.setdefault(k, []).append(i)
            for k in o["writes"]:
                last_w[k] = i
                readers[k] = []
            if o["dma"] is not None:
                if (o["reads"] or o["writes"]) and not o["nobar"]:
                    all_dma_since.add(i)
            else:
                last_on_eng[o["eng"]] = i
        needed = set()
        for d in deps:
            needed |= d
        sems = {}
        sig = [None] * n
        cnt = {}
        for i, o in enumerate(ops):
            if o["bar"]:
                continue
            if o["dma"] is not None:
                sn = "d_" + o["dma"].replace("W:", "").replace(".", "_")
                cnt[sn] = cnt.get(sn, 0) + 16
                sig[i] = (sn, cnt[sn], 16)
            elif i in needed:
                sn = "e_" + o["eng"]
                cnt[sn] = cnt.get(sn, 0) + 1
                sig[i] = (sn, cnt[sn], 1)
        per_eng = {e: [] for e in engs}
        waited = {e: {} for e in engs}
        nwaits = 0
        for i, o in enumerate(ops):
            if o["bar"]:
                continue
            e = o["eng"]
            ws = {}
            for j in deps[i]:
                sn, val, _ = sig[j]
                if o["dma"] is None and ops[j]["dma"] is None and ops[j]["eng"] == e == PE:
                    continue
                if waited[e].get(sn, 0) >= val:
                    continue
                ws[sn] = max(ws.get(sn, 0), val)
            for sn, val in ws.items():
                waited[e][sn] = val
                nwaits += 1
            per_eng[e].append((list(ws.items()), o["emit"], sig[i]))
        finals = [(sn, v) for sn, v in cnt.items()]
        self.stats = dict(nops=n, nwaits=nwaits, nsems=len(cnt),
                          per_eng={e: len(per_eng[e]) for e in engs})
        for sn in cnt:
            sems[sn] = es.enter_context(nc.semaphore("s_" + sn))
        block = es.enter_context(nc.Block())

        def make(ename):
            lst = per_eng[ename]

            def body(eng):
                for ws, emit, sg in lst:
                    for sn, val in ws:
                        eng.wait_ge(sems[sn], val)
                    ins = emit(eng)
                    if sg is not None:
                        ins.then_inc(sems[sg[0]], sg[2])
                if ename == SP:
                    for sn, v in finals:
                        eng.wait_ge(sems[sn], v)
            return body

        block.sync(make(SP))
        block.tensor(make(PE))
        block.scalar(make(ACT))
        block.vector(make(DVE))
        block.gpsimd(make(POOL))


def tile_cols(g, T):
    if g == 0:
        return 128 * T, 1
    if g == 1:
        return 512 * (T % 4) + (T // 4), 4
    return T, 16


def has_prev(g, T):
    if g == 0:
        return T >= 1
    if g == 1:
        return (T % 4) >= 1
    return False


def build_program(nlay=NL):
    nc = bass.Bass("TRN2", target_bir_lowering=False)
    din = lambda name, shape: nc.dram_tensor(name, list(shape), F32, kind="ExternalInput").ap()
    dout = lambda name, shape: nc.dram_tensor(name, list(shape), F32, kind="ExternalOutput").ap()
    x_p = din("x_p", [S, DM])
    x_s = din("x_s", [NS, DM])
    ck = [din("ck%d" % g, [NL, NS, GROUPS[g][0], 256]) for g in range(3)]
    cv = [din("cv%d" % g, [NL, NS, GROUPS[g][0], 256]) for g in range(3)]
    stc = din("stc", [NL, NS, 30, 768])
    w_in = din("w_in", [NL, DM, NIN])
    w_ao = din("w_ao", [NL, 768, DM])
    w_co = din("w_co", [NL, 768, DM])
    w_o = din("w_o", [NL, DM, DM])
    conv_w = din("conv_w", [NL, 31, 768])
    ln_g = din("ln_g", [NL, 768])
    ln_b = din("ln_b", [NL, 768])
    n_pre = din("n_pre", [NL, DM])
    n_post = din("n_post", [NL, DM])
    c_ident = din("c_ident", [128, 128])
    c_mask = din("c_mask", [128, 2, 128])
    c_rope = din("c_rope", [128, 2, 48, 8])
    c_ropes = din("c_ropes", [NS, 2, 8])
    c_sel = din("c_sel", [NS, NS, 128])

    y_p = dout("y_p", [S, DM])
    y_s = dout("y_s", [NS, DM])
    pk = [dout("pk%d" % g, [NL, GROUPS[g][0], 256]) for g in range(3)]
    pv = [dout("pv%d" % g, [NL, GROUPS[g][0], 256]) for g in range(3)]
    pconv = dout("pconv", [NL, 30, 768])
    sk = [dout("sk%d" % g, [NL, NS, GROUPS[g][0], 256]) for g in range(3)]
    sv = [dout("sv%d" % g, [NL, NS, GROUPS[g][0], 256]) for g in range(3)]
    sconv = dout("sconv", [NL, NS, 30, 768])

    es = ExitStack()
    with es:
        P = Prog(nc, es)
        sb = lambda name, shape, dt: es.enter_context(nc.sbuf_tensor(name, list(shape), dt))
        ps = es.enter_context(nc.psum_tensor("ps", [128, 8, 512], F32))

        def bank(i):
            return ps[:, i, :]

        hT = sb("hT", [128, 8, TOK], BF16)
        aT = sb("aT", [128, 6, TOK], BF16)
        ident_f = sb("ident_f", [128, 128], F32)
        ident_b = sb("ident_b", [128, 128], BF16)
        ones_b = sb("ones_b", [128, 128], BF16)
        ones_f = sb("ones_f", [128, 4], F32)
        mask_b = sb("mask_b", [128, 2, 128], BF16)
        rope = sb("rope", [128, 2, 48, 8], F32)
        ropes = sb("ropes", [NS, 2, 8], F32)
        sel_b = sb("sel_b", [NS, NS, 128], BF16)
        gpreT = sb("gpreT", [128, 8], F32)
        gpost = sb("gpost", [128, DM], F32)
        lngT = sb("lngT", [128, NL, 6], F32)
        lnbT = sb("lnbT", [128, NL, 6], F32)
        xs_t = sb("xs_t", [NS, DM], F32)
        sga_s = sb("sga_s", [128, 6, NS], F32)
        small = sb("small", [128, 64], F32)
        u_s = sb("u_s", [128, 6, NS], F32)
        s_cdwf = sb("s_cdwf", [128, 6, NS], F32)
        s_cdwb = sb("s_cdwb", [128, 6, NS], BF16)
        s_cdw2b = sb("s_cdw2b", [128, 6, NS], BF16)
        s_sgc = sb("s_sgc", [128, 6, NS], BF16)
        s_sgt = [sb("s_sgt%d" % i_, [128, NS], F32) for i_ in range(2)]
        s_mean = sb("s_mean", [128, NS], F32)
        s_vv = sb("s_vv", [128, NS], F32)
        s_ybf = [sb("s_ybf%d" % i_, [128, NS], BF16) for i_ in range(2)]
        s_sgl = [sb("s_sgl%d" % i_, [128, NS], BF16) for i_ in range(2)]
        us_tok = sb("us_tok", [128, 768], F32)
        ARENA_B = 131072
        arena = sb("arena", [128, ARENA_B // 4], F32)

        class Carver:
            def __init__(self, start):
                self.off = start

            def take(self, shape, dt):
                esz = 4 if dt == F32 else 2
                nel = int(np.prod(shape))
                nb = (nel * esz + 31) // 32 * 32
                assert self.off % 4 == 0
                assert self.off + nb <= ARENA_B, ("arena overflow", self.off + nb)
                v = arena[:, self.off // 4:(self.off + nb) // 4]
                if dt != F32:
                    v = v.bitcast(dt)
                v = v[:, 0:nel]
                self.off += nb
                if len(shape) == 2:
                    return v.rearrange("p (a b) -> p a b", a=shape[0])
                if len(shape) == 3:
                    return v.rearrange("p (a b c) -> p a b c", a=shape[0], b=shape[1])
                return v

        ln_tok = Carver(0).take([2, 768], F32)
        CT_B = (6 * TOK * 2 + 31) // 32 * 32
        MT_B = (8 * TOK * 2 + 31) // 32 * 32
        cT = Carver(0).take([6, TOK], BF16)
        mT = Carver(CT_B).take([8, TOK], BF16)
        assert CT_B % 32 == 0 and MT_B % 32 == 0

        P.op(SP, lambda e: e.dma_start(out=ident_f[:], in_=c_ident), writes=["c0"], dma="c")
        P.op(SP, lambda e: e.dma_start(out=rope[:], in_=c_rope), writes=["c1"], dma="c")
        P.op(SP, lambda e: e.dma_start(out=ropes[:], in_=c_ropes), writes=["c2"], dma="c")
        P.op(POOL, lambda e: e.dma_start(out=sel_b[:], in_=c_sel), writes=["c3"], dma="c")
        P.op(SP, lambda e: e.dma_start(out=xs_t[:], in_=x_s), writes=["c4"], dma="c")
        P.op(POOL, lambda e: e.dma_start(out=mask_b[:], in_=c_mask), writes=["c5"], dma="c")
        P.op(SP, lambda e: e.dma_start(out=ln_tok[0:NL, 0, :], in_=ln_g), writes=["c6"], dma="c")
        P.op(SP, lambda e: e.dma_start(out=ln_tok[0:NL, 1, :], in_=ln_b), writes=["c7"], dma="c")
        def _ncdma(e, out, in_):
            with nc.allow_non_contiguous_dma(reason="1024 per-feature gains, one-off per layer"):
                return e.dma_start(out=out, in_=in_)
        P.op(SP, lambda e: _ncdma(e, gpreT[:], n_pre[0].rearrange("(k p) -> p k", p=128)), writes=["W:gpre"], dma="W:gpre")
        P.op(DVE, lambda e: e.memset(ones_b[:], 1.0), writes=["ones_b"])
        P.op(DVE, lambda e: e.memset(ones_f[:], 1.0), writes=["ones_f"])
        P.barrier()
        P.op(DVE, lambda e: e.tensor_copy(out=ident_b[:], in_=ident_f[:]), writes=["ident_b"])

        def trln(e):
            for a in range(2):
                for c in range(6):
                    i = e.transpose(out=ps[:, 7, (a * 6 + c) * 4:(a * 6 + c) * 4 + NL], in_=ln_tok[0:NL, a, c * 128:(c + 1) * 128],
                                    identity=ident_f[0:NL, 0:NL])
            return i
        P.op(PE, trln, writes=["bank7"])
        P.op(ACT, lambda e: e.copy(out=lngT[:, :, :].rearrange("p l c -> p c l"), in_=ps[:, 7, 0:24].rearrange("p (c l) -> p c l", c=6)),
             reads=["bank7"], writes=["lngT"])
        P.op(ACT, lambda e: e.copy(out=lnbT[:, :, :].rearrange("p l c -> p c l"), in_=ps[:, 7, 24:48].rearrange("p (c l) -> p c l", c=6)),
             reads=["bank7"], writes=["lnbT"])
        P.barrier()

        P.mark(1)
        def p1_norm(xt_ap, npart, xkey, slot):
            ss = small[0:npart, slot * 4 + 0:slot * 4 + 1]
            sd = small[0:npart, slot * 4 + 1:slot * 4 + 2]
            rs = small[0:npart, slot * 4 + 2:slot * 4 + 3]
            hs = p1_hs[slot]
            ks = "p1ss%d" % slot
            P.op(ACT, lambda e: e.activation(out=p1_junk[0:npart, :], in_=xt_ap, func=AF.Square, accum_out=ss),
                 reads=[xkey], writes=["p1junk", ks])
            P.op(ACT, lambda e: e.activation(out=sd, in_=ss, func=AF.Sqrt, scale=1.0 / DM, bias=eps_rms[0:npart, :]),
                 reads=[ks], writes=[ks + "d"])
            P.op(DVE, lambda e: e.reciprocal(out=rs, in_=sd), reads=[ks + "d"], writes=[ks + "r"])
            P.op(ACT, lambda e: e.activation(out=hs[0:npart, :], in_=xt_ap, func=AF.Identity, scale=rs),
                 reads=[xkey, ks + "r"], writes=["hs%d" % slot])

        def p1_tr(npart, col0, slot, bslot):
            hs = p1_hs[slot]
            trb = bank(4 + bslot).bitcast(BF16)

            def tr(e):
                for kc in range(8):
                    i = e.transpose(out=trb[:, kc * 128:kc * 128 + npart], in_=hs[0:npart, kc * 128:(kc + 1) * 128],
                                    identity=ident_b[0:npart, 0:npart])
                return i
            P.op(PE, tr, reads=["hs%d" % slot], writes=["bank%d" % (4 + bslot)])
            trv = trb.rearrange("p (k t) -> p k t", k=8)
            def evac(e):
                for kc in range(8):
                    i = e.activation(out=hT[:, kc, col0:col0 + npart], in_=trv[:, kc, 0:npart], func=AF.Identity,
                                     scale=gpreT[:, kc:kc + 1])
                return i
            P.op(ACT, evac, reads=["bank%d" % (4 + bslot), "W:gpre"], writes=["hT"])

        eps_rms = sb("eps_rms", [128, 1], F32)
        eps_ln = sb("eps_ln", [128, 1], F32)
        P.op(DVE, lambda e: e.memset(eps_rms[:], RMS_EPS), writes=["eps_rms"])
        P.op(DVE, lambda e: e.memset(eps_ln[:], LN_EPS), writes=["eps_ln"])
        P.barrier()

        c6 = Carver(88192)
        xt = [c6.take([DM], F32) for _ in range(3)]
        p1_hs = [c6.take([DM], BF16) for _ in range(3)]
        p1_junk = c6.take([DM], BF16)
        p6_t1 = c6.take([DM], F32)
        p6_t = [p6_t1, p6_t1]
        assert c6.off <= 114688
        wo_sb = Carver(114688).take([8, DM], BF16)
        cpre = Carver(0)
        wqkv_pre = [cpre.take([8, 384], BF16) for _ in range(2)]
        wga_pre = [cpre.take([8, 128], BF16) for _ in range(3)]

        import os as _os
        NOPF = bool(_os.environ.get("KNOPF"))

        def prefetch_qkv(ln):
            if NOPF:
                return
            for part, coff in enumerate((0, 768, 1536)):
                P.op(POOL, lambda e, part=part, coff=coff: e.dma_start(
                    out=wqkv_pre[0][:, :, part * 128:(part + 1) * 128],
                    in_=w_in[ln, :, coff:coff + 128].rearrange("(k p) c -> p k c", p=128)),
                    writes=["W:wqkv%d.%d" % (0, part)], dma="W:wqkv%d" % 0, nobar=True)
            P.op(POOL, lambda e: e.dma_start(
                out=wga_pre[0][:], in_=w_in[ln, :, 2304:2304 + 128].rearrange("(k p) c -> p k c", p=128)),
                writes=["W:wga%d" % 0], dma="W:wga%d" % 0, nobar=True)

        prefetch_qkv(0)
        for T in range(16):
            sl = T % 3
            P.op(SP, lambda e, T=T, sl=sl: e.dma_start(out=xt[sl][:], in_=x_p[T * 128:(T + 1) * 128, :]),
                 writes=["W:xt%d" % sl], dma="W:xt%d" % sl)
            p1_norm(xt[sl][:], 128, "W:xt%d" % sl, T % 3)
            if T >= 1:
                p1_tr(128, (T - 1) * 128, (T - 1) % 3, (T - 1) % 2)
        p1_norm(xs_t[:], NS, "xs_t", 16 % 3)
        p1_tr(128, 15 * 128, 15 % 3, 15 % 2)
        p1_tr(NS, S, 16 % 3, 0)
        P.barrier()

        TCH = [(0, 512), (512, 512), (1024, 512), (1536, 512), (S, NS)]

        def emit_copies(l, gate=()):
            for g in range(3):
                W = GROUPS[g][0]
                for src, dst in ((ck[g], sk[g]), (cv[g], sv[g])):
                    for b in range(NS):
                        P.op(SP, lambda e, src=src, dst=dst, W=W, b=b: e.dma_start(
                            out=dst[l, b, 0:W - 1, :].rearrange("w c -> (w c)").rearrange("(s e) -> s e", s=16),
                            in_=src[l, b, 1:W, :].rearrange("w c -> (w c)").rearrange("(s e) -> s e", s=16)), reads=gate, dma="cp", nobar=True)
            for b in range(NS):
                P.op(SP, lambda e, b=b: e.dma_start(
                    out=sconv[l, b, 0:29, :].rearrange("w c -> (w c)").rearrange("(s e) -> s e", s=16),
                    in_=stc[l, b, 1:30, :].rearrange("w c -> (w c)").rearrange("(s e) -> s e", s=16)), dma="cp")

        def layer(l):
            last = (l == nlay - 1)
            P.mark(2)
            c2 = Carver(0)
            qb = c2.take([NS, 768], F32)
            Pb = c2.take([NS, 768], F32)
            p3_small = c2.take([512], F32)
            pnx = c2.take([3, 768], F32)
            PVb = c2.take([NS, 768], BF16)
            qs_bf = c2.take([6, 128], BF16)
            p3_end = c2.off
            ckv = Carver(105472)
            Kc = ckv.take([NS, 768], F32)
            Vc = ckv.take([NS, 768], F32)
            c2 = Carver(0)
            wqkv = [c2.take([8, 384], BF16) for _ in range(2)]
            wga = [c2.take([8, 128], BF16) for _ in range(3)]
            stage = [c2.take([384], F32) for _ in range(3)]
            rt = [c2.take([4, 4, 8], F32) for _ in range(3)]
            QK = [c2.take([2, S], BF16) for _ in range(2)]
            Vb = [c2.take([16, 128], BF16) for _ in range(2)]
            OT = c2.take([3, S], F32)
            Dn = c2.take([S], F32)
            rD = c2.take([S], F32)
            pt = [[c2.take([256], BF16) for _ in range(2)] for _ in range(2)]
            sga = [c2.take([512], BF16) for _ in range(2)]
            tt1 = c2.take([512], F32)
            tt = [tt1, tt1]
            c2.off = max(c2.off, p3_end)
            qkvs = c2.take([6, 384], F32)
            assert c2.off <= 105472, c2.off
            WCONV_OFF = 59392
            assert p3_end <= WCONV_OFF and c2.off - 9216 >= WCONV_OFF + 36864, (p3_end, c2.off)
            wconv = Carver(WCONV_OFF).take([8, 2304], BF16)

            for g in range(3):
                dil = GROUPS[g][1]
                L = GROUPS[g][0]
                P.op(SP, lambda e, g=g, dil=dil, L=L: e.dma_start(
                    out=Kc[:, :, g * 256:(g + 1) * 256], in_=ck[g][l, :, 0:L:dil, :].rearrange("b p c -> p b c")),
                    writes=["W:Kc%d" % g], dma="kc", nobar=True)
                P.op(SP, lambda e, g=g, dil=dil, L=L: e.dma_start(
                    out=Vc[:, :, g * 256:(g + 1) * 256], in_=cv[g][l, :, 0:L:dil, :].rearrange("b p c -> p b c")),
                    writes=["W:Vc%d" % g], dma="vc", nobar=True)
            it = 0
            for sp in range(2):
                for g in range(3):
                    P.mark(2.0)
                    gi = 2 * g + sp
                    wsl = it % 2
                    qsl = it % 2
                    it += 1
                    c0 = g * 256 + sp * 128
                    for part, coff in enumerate((c0, 768 + c0, 1536 + c0)):
                        if it == 1 and not NOPF:
                            break
                        P.op(POOL, lambda e, wsl=wsl, part=part, coff=coff: e.dma_start(
                            out=wqkv[wsl][:, :, part * 128:(part + 1) * 128],
                            in_=w_in[l, :, coff:coff + 128].rearrange("(k p) c -> p k c", p=128)),
                            writes=["W:wqkv%d.%d" % (wsl, part)], dma="wqkv%d" % wsl)
                    if it != 1 or NOPF:
                        P.op(POOL, lambda e, g=g, gi=gi: e.dma_start(
                            out=wga[g][:], in_=w_in[l, :, 2304 + gi * 128:2304 + (gi + 1) * 128].rearrange("(k p) c -> p k c", p=128)),
                            writes=["W:wga%d" % g], dma="W:wga%d" % g)
                    wkeys = ["W:wqkv%d.%d" % (wsl, p_) for p_ in range(3)]
                    def stage_AB(T, g=g, sp=sp, gi=gi, wsl=wsl, wkeys=wkeys):
                        npart = 128 if T < 16 else NS
                        ssl = T % 3
                        bq = T % 2
                        if T < 16:
                            base, stride = tile_cols(g, T)
                            lcols = lambda kc: hT[:, kc, base:base + 127 * stride + 1:stride]
                        else:
                            base, stride = 0, 1
                            lcols = lambda kc: hT[:, kc, S:S + NS]
                        P.mark(2.1 if T < 16 else 2.15)

                        def mmq(e):
                            for kc in range(8):
                                i = e.matmul(ps[0:npart, bq, 0:384], lhsT=lcols(kc), rhs=wqkv[wsl][:, kc, :],
                                             start=(kc == 0), stop=(kc == 7))
                            return i
                        P.op(PE, mmq, reads=["hT"] + wkeys, writes=["bank%d" % bq])
                        if T < 16:
                            dst = stage[ssl][:, :]
                            skey = "stage%d" % ssl
                        else:
                            dst = qkvs[0:NS, gi, :]
                            skey = "qkvs%d" % gi
                        P.op(ACT, lambda e: e.copy(out=dst, in_=ps[0:npart, bq, 0:384]),
                             reads=["bank%d" % bq], writes=[skey])
                        qk4 = dst[:, 0:256].rearrange("p (a d) -> p a d", a=4)
                        x1 = qk4[:, :, 0:8]
                        x2 = qk4[:, :, 8:16]
                        if T < 16:
                            cosv = rope[:, 0, g * 16 + T:g * 16 + T + 1, :].to_broadcast([128, 4, 8])
                            sinv = rope[:, 1, g * 16 + T:g * 16 + T + 1, :].to_broadcast([128, 4, 8])
                        else:
                            cosv = ropes[:, 0:1, :].to_broadcast([NS, 4, 8])
                            sinv = ropes[:, 1:2, :].to_broadcast([NS, 4, 8])
                        r = rt[ssl]
                        rk = "rt%d" % ssl
                        for ti, (a_, b_) in enumerate(((x1, cosv), (x2, sinv), (x2, cosv), (x1, sinv))):
                            P.op(DVE, lambda e, ti=ti, a_=a_, b_=b_: e.tensor_tensor(
                                out=r[0:npart, ti, :, :], in0=a_, in1=b_, op=ALU.mult),
                                reads=[skey], writes=[rk + ".%d" % ti])
                        P.op(DVE, lambda e: e.tensor_tensor(
                            out=x1, in0=r[0:npart, 0, :, :], in1=r[0:npart, 1, :, :], op=ALU.subtract),
                            reads=[rk + ".0", rk + ".1", rk + ".2", rk + ".3"], writes=[skey])
                        P.op(DVE, lambda e: e.tensor_tensor(
                            out=x2, in0=r[0:npart, 2, :, :], in1=r[0:npart, 3, :, :], op=ALU.add),
                            reads=[rk + ".2", rk + ".3", skey], writes=[skey, rk + ".0", rk + ".1", rk + ".2", rk + ".3"])
                        if T == 16:
                            return
                        P.mark(2.2)
                        W, dil = GROUPS[g]
                        if (g == 0 and T == 15) or (g == 1 and T % 4 == 3) or g == 2:
                            row0 = base - (S - W)
                            P.op(SP, lambda e: e.dma_start(
                                out=pk[g][l, row0:row0 + 127 * stride + 1:stride, sp * 128:(sp + 1) * 128],
                                in_=stage[ssl][:, 128:256]), reads=[skey], dma="pkv%d" % ssl)
                            P.op(SP, lambda e: e.dma_start(
                                out=pv[g][l, row0:row0 + 127 * stride + 1:stride, sp * 128:(sp + 1) * 128],
                                in_=stage[ssl][:, 256:384]), reads=[skey], dma="pkv%d" % ssl)

                    def stage_C(T, qsl=qsl):
                        ssl = T % 3
                        bt = 2 + T % 2
                        skey = "stage%d" % ssl
                        P.mark(2.3)

                        def trq(e):
                            e.transpose(out=ps[:, bt, 0:128], in_=stage[ssl][:, 0:128], identity=ident_f[:])
                            return e.transpose(out=ps[:, bt, 128:256], in_=stage[ssl][:, 128:256], identity=ident_f[:])
                        P.op(PE, trq, reads=[skey], writes=["bank%d" % bt])
                        P.op(ACT, lambda e: e.copy(
                            out=QK[qsl][:, :, T * 128:(T + 1) * 128],
                            in_=ps[:, bt, 0:256].rearrange("p (a t) -> p a t", a=2)),
                            reads=["bank%d" % bt], writes=["QK%d.%d" % (qsl, T)])
                        P.op(DVE, lambda e: e.tensor_copy(out=Vb[qsl][:, T, :], in_=stage[ssl][:, 256:384]),
                             reads=[skey], writes=["Vb%d.%d" % (qsl, T)])

                    for T in range(17 + 2):
                        if T < 17:
                            stage_AB(T)
                        if T >= 2 and T - 2 < 16:
                            stage_C(T - 2)

                    def stage_S(T, g=g, qsl=qsl):
                        par = T % 2
                        kts = [T] + ([T - 1] if has_prev(g, T) else [])
                        nk = len(kts)
                        qkeys = ["QK%d.%d" % (qsl, t_) for t_ in kts]
                        for h in range(2):
                            P.mark(2.41)
                            bS = 4 + 2 * h + par

                            def mms(e, h=h, bS=bS):
                                for i_, kt in enumerate(kts):
                                    i = e.matmul(ps[:, bS, i_ * 128:(i_ + 1) * 128],
                                                 lhsT=QK[qsl][64 * h:64 * h + 64, 1, kt * 128:(kt + 1) * 128],
                                                 rhs=QK[qsl][64 * h:64 * h + 64, 0, T * 128:(T + 1) * 128],
                                                 start=True, stop=True)
                                return i
                            P.op(PE, mms, reads=qkeys, writes=["bank%d" % bS])
                            ptk = "pt%d%d" % (h, par)
                            P.op(ACT, lambda e, h=h, bS=bS: e.activation(
                                out=pt[h][par][:, 0:nk * 128], in_=ps[:, bS, 0:nk * 128], func=AF.Exp, scale=0.125),
                                reads=["bank%d" % bS], writes=[ptk])
                            P.mark(2.42)
                            P.op(DVE, lambda e, h=h: e.tensor_tensor(
                                out=pt[h][par][:, 0:nk * 128], in0=pt[h][par][:, 0:nk * 128],
                                in1=mask_b[:, 0:nk, :].rearrange("p a b -> p (a b)"), op=ALU.mult),
                                reads=[ptk], writes=[ptk])

                    def stage_O(T, g=g, qsl=qsl):
                        par = T % 2
                        kts = [T] + ([T - 1] if has_prev(g, T) else [])
                        nk_ = len(kts)
                        base, stride = tile_cols(g, T)
                        vkeys = ["Vb%d.%d" % (qsl, t_) for t_ in kts]
                        bO = 2 + par
                        P.mark(2.43)

                        def mmo(e):
                            for h in range(2):
                                for i_, kt in enumerate(kts):
                                    e.matmul(ps[64 * h:64 * h + 64, bO, 0:128], lhsT=Vb[qsl][:, kt, 64 * h:64 * h + 64],
                                             rhs=pt[h][par][:, i_ * 128:(i_ + 1) * 128], start=(i_ == 0), stop=(i_ == nk_ - 1))
                                for i_, kt in enumerate(kts):
                                    i = e.matmul(ps[64 * h:64 * h + 64, bO, 128:256], lhsT=ones_b[:, 0:64],
                                                 rhs=pt[h][par][:, i_ * 128:(i_ + 1) * 128], start=(i_ == 0), stop=(i_ == nk_ - 1))
                            return i
                        P.op(PE, mmo, reads=vkeys + ["pt0%d" % par, "pt1%d" % par], writes=["bank%d" % bO])
                        P.mark(2.44)
                        P.op(ACT, lambda e: e.copy(
                            out=OT[:, g, base:base + 127 * stride + 1:stride], in_=ps[:, bO, 0:128]),
                            reads=["bank%d" % bO], writes=["OT%d" % g])
                        if g == 0:
                            P.op(DVE, lambda e: e.tensor_copy(
                                out=Dn[:, base:base + 127 * stride + 1:stride], in_=ps[:, bO, 128:256]),
                                reads=["bank%d" % bO], writes=["Dn"])
                        else:
                            P.op(DVE, lambda e: e.tensor_tensor(
                                out=Dn[:, base:base + 127 * stride + 1:stride], in0=ps[:, bO, 128:256],
                                in1=Dn[:, base:base + 127 * stride + 1:stride], op=ALU.add),
                                reads=["bank%d" % bO, "Dn"], writes=["Dn"])

                    P.mark(2.4)
                    for T in range(17):
                        if T < 16:
                            stage_S(T)
                        if T >= 1:
                            stage_O(T - 1)
                P.mark(2.5)
                for k4 in range(4):
                    P.op(DVE, lambda e, k4=k4: e.reciprocal(out=rD[:, k4 * 512:(k4 + 1) * 512], in_=Dn[:, k4 * 512:(k4 + 1) * 512]),
                         reads=["Dn"], writes=["rD%d" % k4])
                ci = 0
                for g in range(3):
                    gi = 2 * g + sp
                    for (t0, n_) in TCH:
                        par = ci % 2
                        ci += 1
                        bG = par

                        def mmg(e, g=g, t0=t0, n_=n_, bG=bG):
                            for kc in range(8):
                                i = e.matmul(ps[:, bG, 0:n_], lhsT=wga[g][:, kc, :], rhs=hT[:, kc, t0:t0 + n_],
                                             start=(kc == 0), stop=(kc == 7))
                            return i
                        P.op(PE, mmg, reads=["hT", "W:wga%d" % g], writes=["bank%d" % bG])
                        if t0 == S:
                            P.op(ACT, lambda e, gi=gi, bG=bG: e.activation(out=sga_s[:, gi, :], in_=ps[:, bG, 0:NS], func=AF.Silu),
                                 reads=["bank%d" % bG], writes=["sga_s%d" % gi])
                            continue
                        P.op(ACT, lambda e, par=par, bG=bG: e.activation(out=sga[par][:, :], in_=ps[:, bG, 0:512], func=AF.Silu),
                             reads=["bank%d" % bG], writes=["sga%d" % par])
                        P.op(DVE, lambda e, g=g, t0=t0, par=par: e.tensor_tensor(
                            out=tt[par][:, :], in0=OT[:, g, t0:t0 + 512], in1=rD[:, t0:t0 + 512], op=ALU.mult),
                            reads=["OT%d" % g, "rD%d" % (t0 // 512)], writes=["tt"])
                        P.op(DVE, lambda e, gi=gi, t0=t0, par=par: e.tensor_tensor(
                            out=aT[:, gi, t0:t0 + 512], in0=tt[par][:, :], in1=sga[par][:, :], op=ALU.mult),
                            reads=["tt", "sga%d" % par], writes=["aT"])
            P.barrier()
            P.mark(3)
            for g in range(3):
                L = GROUPS[g][0]
                P.op(SP, lambda e, g=g, L=L: e.dma_start(
                    out=sk[g][l, :, L - 1, :].rearrange("b (s c) -> b s c", s=2), in_=qkvs[0:NS, 2 * g:2 * g + 2, 128:256]),
                    reads=["qkvs%d" % (2 * g), "qkvs%d" % (2 * g + 1)], dma="sknew")
                P.op(SP, lambda e, g=g, L=L: e.dma_start(
                    out=sv[g][l, :, L - 1, :].rearrange("b (s c) -> b s c", s=2), in_=qkvs[0:NS, 2 * g:2 * g + 2, 256:384]),
                    reads=["qkvs%d" % (2 * g), "qkvs%d" % (2 * g + 1)], dma="sknew")
            for part in range(3):
                P.op(POOL, lambda e, part=part: e.dma_start(
                    out=wconv[:, :, part * 768:(part + 1) * 768],
                    in_=w_in[l, :, 3072 + part * 768:3072 + (part + 1) * 768].rearrange("(k p) c -> p k c", p=128)),
                    writes=["W:wconv%d" % part], dma="wconv", nobar=True)
            qall = ["qkvs%d" % i_ for i_ in range(6)]
            P.op(ACT, lambda e: e.copy(out=qs_bf[0:NS, :, :], in_=qkvs[0:NS, :, 0:128]), reads=qall, writes=["qs_bf"])
            for b in range(NS):
                def mmb(e, b=b):
                    e.matmul(ps[:, 2, 0:512], lhsT=sel_b[:, b, :], rhs=qs_bf[0:NS, 0:4, :], start=True, stop=True)
                    return e.matmul(ps[:, 3, 0:256], lhsT=sel_b[:, b, :], rhs=qs_bf[0:NS, 4:6, :], start=True, stop=True)
                P.op(PE, mmb, reads=["qs_bf"], writes=["bank2", "bank3"])
                P.op(ACT, lambda e, b=b: e.copy(out=qb[:, b, :], in_=ps[:, 2:4, :].rearrange("p a b -> p (a b)")[:, 0:768]),
                     reads=["bank2", "bank3"], writes=["qb%d" % b])
            qbk = ["qb%d" % b for b in range(NS)]
            P.op(DVE, lambda e: e.tensor_tensor(out=qb[:, :, :], in0=Kc[:, :, :], in1=qb[:, :, :], op=ALU.mult),
                 reads=qbk + ["W:Kc0", "W:Kc1", "W:Kc2"], writes=["prod"])
            Ssc = p3_small[:, 0:48]
            P.op(DVE, lambda e: e.tensor_reduce(out=Ssc, in_=qb[:, :, :].rearrange("p b (h d) -> p (b h) d", d=64),
                                                axis=AX.X, op=ALU.add), reads=["prod"], writes=["Ssc"])
            Pbb = Pb[:, :, :].rearrange("p b c -> p (b c)")[:, 0:1536].bitcast(BF16).rearrange("p (b c) -> p b c", b=NS)
            P.op(ACT, lambda e: e.activation(out=Pbb.rearrange("p b (h d) -> p (b h) d", d=64),
                                             in_=Ssc.rearrange("p (a o) -> p a o", o=1).to_broadcast([128, 48, 64]),
                                             func=AF.Exp, scale=0.125), reads=["Ssc"], writes=["Pb"])
            P.op(DVE, lambda e: e.tensor_tensor(out=PVb[:, :, :], in0=Vc[:, :, :], in1=Pbb, op=ALU.mult),
                 reads=["Pb", "W:Vc0", "W:Vc1", "W:Vc2"], writes=["PV"])
            sn = p3_small[0:NS, 48:60]
            pn3 = pnx[0:NS, 0, :].rearrange("p (a c) -> p a c", a=6)
            pbn = pnx[0:NS, 1, :].bitcast(BF16)[:, 0:768].rearrange("p (a c) -> p a c", a=6)
            pvn = pnx[0:NS, 2, :].bitcast(BF16)[:, 0:768].rearrange("p (a c) -> p a c", a=6)
            P.op(DVE, lambda e: e.tensor_tensor(out=pn3, in0=qkvs[0:NS, :, 0:128], in1=qkvs[0:NS, :, 128:256], op=ALU.mult),
                 reads=qall, writes=["pn3"])
            P.op(DVE, lambda e: e.tensor_reduce(out=sn, in_=pn3.rearrange("p a (h d) -> p (a h) d", d=64), axis=AX.X, op=ALU.add),
                 reads=["pn3"], writes=["sn"])
            P.op(ACT, lambda e: e.activation(out=pbn.rearrange("p a (h d) -> p (a h) d", d=64),
                                             in_=sn.rearrange("p (a o) -> p a o", o=1).to_broadcast([NS, 12, 64]),
                                             func=AF.Exp, scale=0.125), reads=["sn"], writes=["pbn"])
            P.op(DVE, lambda e: e.tensor_tensor(out=pvn, in0=qkvs[0:NS, :, 256:384], in1=pbn, op=ALU.mult),
                 reads=qall + ["pbn"], writes=["pvn"])

            def mmsa(e):
                for c in range(6):
                    e.matmul(ps[:, 4, c * 4:c * 4 + 4], lhsT=pvn[:, c, :], rhs=ident_b[0:NS, 0:NS], start=True, stop=False,
                             skip_group_check=True)
                    for b in range(NS):
                        e.matmul(ps[:, 4, c * 4 + b:c * 4 + b + 1], lhsT=PVb[:, b, c * 128:(c + 1) * 128], rhs=ones_b[:, 0:1],
                                 start=False, stop=(b == NS - 1), skip_group_check=True)
                for c in range(6):
                    e.matmul(ps[:, 5, c * 4:c * 4 + 4], lhsT=pbn[:, c, :], rhs=ident_b[0:NS, 0:NS], start=True, stop=False,
                             skip_group_check=True)
                    for b in range(NS):
                        i = e.matmul(ps[:, 5, c * 4 + b:c * 4 + b + 1], lhsT=Pbb[:, b, c * 128:(c + 1) * 128], rhs=ones_b[:, 0:1],
                                     start=False, stop=(b == NS - 1), skip_group_check=True)
                return i
            P.op(PE, mmsa, reads=["PV", "Pb", "pvn", "pbn"], writes=["bank4", "bank5"])
            dn = p3_small[:, 64:88]
            Ds = p3_small[:, 96:104]
            as1 = p3_small[:, 128:152]
            P.op(ACT, lambda e: e.copy(out=dn, in_=ps[:, 5, 0:24]), reads=["bank5"], writes=["dn"])
            P.op(DVE, lambda e: e.tensor_tensor(out=Ds, in0=dn[:, 0:8], in1=dn[:, 8:16], op=ALU.add), reads=["dn"], writes=["Ds"])
            P.op(DVE, lambda e: e.tensor_tensor(out=Ds, in0=Ds, in1=dn[:, 16:24], op=ALU.add), reads=["dn", "Ds"], writes=["Ds"])
            P.op(DVE, lambda e: e.reciprocal(out=Ds, in_=Ds), reads=["Ds"], writes=["Ds"])
            P.op(DVE, lambda e: e.tensor_tensor(out=as1.rearrange("p (g x) -> p g x", g=3),
                                                in0=ps[:, 4, 0:24].rearrange("p (g x) -> p g x", g=3),
                                                in1=Ds.rearrange("p (o x) -> p o x", o=1).to_broadcast([128, 3, 8]), op=ALU.mult),
                 reads=["bank4", "Ds"], writes=["as1"])
            P.op(DVE, lambda e: e.tensor_tensor(out=aT[:, :, S:S + NS], in0=as1.rearrange("p (c b) -> p c b", c=6),
                                                in1=sga_s[:, :, :], op=ALU.mult),
                 reads=["as1"] + ["sga_s%d" % i_ for i_ in range(6)], writes=["aT"])
            P.barrier()
            P.mark(4)
            c4 = Carver(CT_B)
            diag = [c4.take([31, 128], BF16) for _ in range(2)]
            cdw_f = c4.take([6, 512], F32)
            u_bf = c4.take([6, 542], BF16)
            assert c4.off <= WCONV_OFF, c4.off
            c4 = Carver(WCONV_OFF + 36864)
            cdw_b = c4.take([6, 512], BF16)
            cdw2_b = c4.take([6, 512], BF16)
            sgc = c4.take([6, 512], BF16)
            sgt = [c4.take([512], F32) for _ in range(2)]
            mean = c4.take([512], F32)
            vv = c4.take([512], F32)
            ybf = [c4.take([512], BF16) for _ in range(2)]
            u_last = c4.take([6, 30], F32)
            cw_tok = c4.take([768], F32)
            cwT = c4.take([6, 31], F32)
            wck = ["W:wconv0", "W:wconv1", "W:wconv2"]
            sgl = [cw_tok[:, 0:256].bitcast(BF16), cw_tok[:, 256:512].bitcast(BF16)]
            cdws0 = small[:, 16:40]
            st = cdw_f[0:30, :, :].rearrange("p a b -> p (a b)").rearrange("p (b c) -> p b c", b=NS)

            P.op(SP, lambda e: e.dma_start(out=cw_tok[0:31, :], in_=conv_w[l]), writes=["cw_tok"], dma="cw")
            P.op(SP, lambda e: e.dma_start(out=st, in_=stc[l].rearrange("b j c -> j b c")), writes=["cdw_f"], dma="st")
            emit_copies(l, gate=wck)

            def trcw(e):
                for c in range(6):
                    i = e.transpose(out=ps[:, 7, c * 32:c * 32 + 31], in_=cw_tok[0:31, c * 128:(c + 1) * 128],
                                    identity=ident_f[0:31, 0:31])
                return i
            P.op(PE, trcw, reads=["cw_tok"], writes=["bank7"])
            P.op(ACT, lambda e: e.copy(out=cwT[:, :, :], in_=ps[:, 7, 0:192].rearrange("p (c j) -> p c j", c=6)[:, :, 0:31]),
                 reads=["bank7"], writes=["cwT"])
            P.op(DVE, lambda e: e.memset(u_bf[:, :, 0:30], 0.0), writes=["u%d" % c for c in range(6)])
            P.op(DVE, lambda e: e.tensor_tensor(out=st, in0=st,
                                                in1=cw_tok[0:30, :].rearrange("p (o c) -> p o c", o=1).to_broadcast([30, NS, 768]),
                                                op=ALU.mult), reads=["cdw_f", "cw_tok"], writes=["cdw_f"])

            def mmsc(e):
                for c in range(6):
                    for b in range(NS):
                        i = e.matmul(ps[:, 6, c * 4 + b:c * 4 + b + 1], lhsT=st[:, b, c * 128:(c + 1) * 128], rhs=ones_f[0:30, 0:1],
                                     start=True, stop=True, skip_group_check=True)
                return i
            P.op(PE, mmsc, reads=["cdw_f"], writes=["bank6"])
            P.op(ACT, lambda e: e.copy(out=cdws0, in_=ps[:, 6, 0:24]), reads=["bank6"], writes=["cdws0"])

            def main_ctx(t0):
                return dict(t0=t0, k=t0 // 512, n=512, samp=False, pfx="", cdw_f=cdw_f, cdw_b=cdw_b, cdw2_b=cdw2_b, sgc=sgc, sgt=sgt,
                            mean=mean, vv=vv, ybf=ybf, sgl=sgl)
            s_ctx = dict(t0=S, k=4, n=NS, samp=True, pfx="s_", cdw_f=s_cdwf, cdw_b=s_cdwb, cdw2_b=s_cdw2b, sgc=s_sgc, sgt=s_sgt,
                         mean=s_mean, vv=s_vv, ybf=s_ybf, sgl=s_sgl)

            def projAB(X, c):
                t0, n_, px = X["t0"], X["n"], X["pfx"]
                sp_ = c % 2
                sg = X["sgt"][sp_]
                if X["samp"]:
                    pa = ps[:, 7, 200 + c * 8:200 + c * 8 + 4]
                    pb = ps[:, 7, 204 + c * 8:204 + c * 8 + 4]
                    bka = bkb = "bank7"
                else:
                    ba = 2 * (c % 2)
                    pa = ps[:, ba, 0:512]
                    pb = ps[:, ba + 1, 0:512]
                    bka, bkb = "bank%d" % ba, "bank%d" % (ba + 1)

                def mmp(e):
                    for dst, off in ((pa, 0), (pb, 768)):
                        for kc in range(8):
                            i = e.matmul(dst, lhsT=wconv[:, kc, off + c * 128:off + (c + 1) * 128],
                                         rhs=hT[:, kc, t0:t0 + n_], start=(kc == 0), stop=(kc == 7), skip_group_check=True)
                    return i
                P.op(PE, mmp, reads=["hT"] + wck, writes=list({bka, bkb}))
                P.op(ACT, lambda e: e.activation(out=sg[:, 0:n_], in_=pb, func=AF.Sigmoid),
                     reads=[bkb], writes=[px + "sgt%d" % sp_])
                if X["samp"]:
                    P.op(DVE, lambda e: e.tensor_tensor(out=u_s[:, c, :], in0=pa, in1=sg[:, 0:NS], op=ALU.mult),
                         reads=[bka, px + "sgt%d" % sp_], writes=["u_s%d" % c])
                    P.op(DVE, lambda e: e.scalar_tensor_tensor(
                        out=s_cdwf[:, c, :], in0=u_s[:, c, :], scalar=cwT[:, c, 30:31], in1=cdws0[:, c * 4:(c + 1) * 4],
                        op0=ALU.mult, op1=ALU.add),
                        reads=["u_s%d" % c, "cwT", "cdws0"], writes=["s_cdwf%d" % c])
                    P.op(DVE, lambda e: e.tensor_copy(out=s_cdwb[:, c, :], in_=s_cdwf[:, c, :]),
                         reads=["s_cdwf%d" % c], writes=["s_cdwb%d" % c])
                    P.op(ACT, lambda e: e.activation(out=s_cdw2b[:, c, :], in_=s_cdwf[:, c, :], func=AF.Square),
                         reads=["s_cdwf%d" % c], writes=["s_cdw2b%d" % c])
                    return
                P.op(DVE, lambda e: e.tensor_tensor(out=u_bf[:, c, 30:542], in0=pa, in1=sg[:, :], op=ALU.mult),
                     reads=[bka, "sgt%d" % sp_], writes=["u%d" % c])
                if t0 == 1536:
                    P.op(DVE, lambda e: e.tensor_tensor(out=u_last[:, c, :], in0=ps[:, 2 * (c % 2), 482:512],
                                                        in1=sg[:, 482:512], op=ALU.mult),
                         reads=[bka, "sgt%d" % sp_], writes=["u_last%d" % c])
                dsl = c % 2
                P.op(DVE, lambda e: e.tensor_tensor(
                    out=diag[dsl][:, :, :], in0=ident_b[:, :].rearrange("p (o q) -> p o q", o=1).to_broadcast([128, 31, 128]),
                    in1=cwT[:, c, :].rearrange("p (j o) -> p j o", o=1).to_broadcast([128, 31, 128]), op=ALU.mult),
                    reads=["cwT", "ident_b"], writes=["diag%d" % dsl])

            def convc(X, c):
                dsl = c % 2
                bD = 4 + dsl
                first = (X["t0"] == 0)

                def mmc(e):
                    for j in range(31):
                        i = e.matmul(ps[:, bD, :], lhsT=diag[dsl][:, j, :], rhs=u_bf[:, c, j:j + 512],
                                     start=(j == 0), stop=(j == 30))
                    return i
                P.op(PE, mmc, reads=["diag%d" % dsl, "u%d" % c], writes=["bank%d" % bD])
                P.op(ACT, lambda e: e.copy(out=cdw_f[:, c, :], in_=ps[:, bD, :]),
                     reads=["bank%d" % bD], writes=["cdwf%d" % c] + (["cdw_f"] if first else []))
                P.op(DVE, lambda e: e.tensor_copy(out=cdw_b[:, c, :], in_=ps[:, bD, :]),
                     reads=["bank%d" % bD], writes=["cdwb%d" % c])
                P.op(ACT, lambda e: e.activation(out=cdw2_b[:, c, :], in_=ps[:, bD, :], func=AF.Square),
                     reads=["bank%d" % bD], writes=["cdw2b%d" % c])
                P.op(DVE, lambda e: e.tensor_copy(out=u_bf[:, c, 0:30], in_=u_bf[:, c, 512:542]),
                     reads=["u%d" % c, "cdwb%d" % c], writes=["u%d" % c])

            def projG(X, c):
                t0, n_, px = X["t0"], X["n"], X["pfx"]
                sp_ = c % 2
                sg = X["sgt"][sp_]
                if X["samp"]:
                    pg = ps[:, 7, 256 + c * 4:256 + c * 4 + 4]
                    bkg = "bank7"
                else:
                    pg = ps[:, c % 4, 0:512]
                    bkg = "bank%d" % (c % 4)

                def mmg4(e):
                    for kc in range(8):
                        i = e.matmul(pg, lhsT=wconv[:, kc, 1536 + c * 128:1536 + (c + 1) * 128],
                                     rhs=hT[:, kc, t0:t0 + n_], start=(kc == 0), stop=(kc == 7), skip_group_check=True)
                    return i
                P.op(PE, mmg4, reads=["hT"] + wck, writes=[bkg])
                P.op(ACT, lambda e: e.activation(out=sg[:, 0:n_], in_=pg, func=AF.Sigmoid),
                     reads=[bkg], writes=[px + "sgt%d" % sp_])
                P.op(DVE, lambda e: e.tensor_tensor(out=X["sgc"][:, c, 0:n_], in0=pg, in1=sg[:, 0:n_], op=ALU.mult),
                     reads=[bkg, px + "sgt%d" % sp_], writes=[px + "sgc%d" % c])

            def ln_head(X):
                n_, px = X["n"], X["pfx"]
                mean_, vv_ = X["mean"], X["vv"]
                if X["samp"]:
                    p5 = ps[:, 7, 300:304]
                    p6 = ps[:, 7, 304:308]
                    bk5 = bk6 = "bank7"
                else:
                    p5 = ps[:, 5, 0:512]
                    p6 = ps[:, 6, 0:512]
                    bk5, bk6 = "bank5", "bank6"

                def mmst(e):
                    for c in range(6):
                        e.matmul(p5, lhsT=ones_b[:, :], rhs=X["cdw_b"][:, c, 0:n_], start=(c == 0), stop=(c == 5), skip_group_check=True)
                    for c in range(6):
                        i = e.matmul(p6, lhsT=ones_b[:, :], rhs=X["cdw2_b"][:, c, 0:n_], start=(c == 0), stop=(c == 5), skip_group_check=True)
                    return i
                P.op(PE, mmst, reads=[px + "cdwb%d" % c for c in range(6)] + [px + "cdw2b%d" % c for c in range(6)], writes=list({bk5, bk6}))
                P.op(ACT, lambda e: e.mul(out=mean_[:, 0:n_], in_=p5, mul=1.0 / 768), reads=[bk5], writes=[px + "mean"])
                P.op(DVE, lambda e: e.tensor_tensor(out=vv_[:, 0:n_], in0=mean_[:, 0:n_], in1=mean_[:, 0:n_], op=ALU.mult),
                     reads=[px + "mean"], writes=[px + "vv"])
                P.op(DVE, lambda e: e.scalar_tensor_tensor(out=vv_[:, 0:n_], in0=p6, scalar=1.0 / 768, in1=vv_[:, 0:n_],
                                                           op0=ALU.mult, op1=ALU.subtract),
                     reads=[bk6, px + "vv"], writes=[px + "vv"])
                P.op(ACT, lambda e: e.activation(out=vv_[:, 0:n_], in_=vv_[:, 0:n_], func=AF.Sqrt, bias=eps_ln[:, :]),
                     reads=[px + "vv"], writes=[px + "vv"])
                P.op(DVE, lambda e: e.reciprocal(out=vv_[:, 0:n_], in_=vv_[:, 0:n_]), reads=[px + "vv"], writes=[px + "vv"])

            def ln_apply(X, c):
                t0, n_, px = X["t0"], X["n"], X["pfx"]
                ysl = c % 2
                cf = X["cdw_f"][:, c, 0:n_]
                yb = X["ybf"][ysl][:, 0:n_]
                sl_ = X["sgl"][ysl][:, 0:n_]
                P.op(DVE, lambda e: e.tensor_tensor(out=cf, in0=cf, in1=X["mean"][:, 0:n_], op=ALU.subtract),
                     reads=[px + "cdwf%d" % c, px + "mean"], writes=[px + "cdwf%d" % c])
                P.op(DVE, lambda e: e.tensor_tensor(out=cf, in0=cf, in1=X["vv"][:, 0:n_], op=ALU.mult),
                     reads=[px + "cdwf%d" % c, px + "vv"], writes=[px + "cdwf%d" % c])
                P.op(ACT, lambda e: e.activation(out=yb, in_=cf, func=AF.Identity, scale=lngT[:, l, c:c + 1], bias=lnbT[:, l, c:c + 1]),
                     reads=[px + "cdwf%d" % c], writes=[px + "ybf%d" % ysl])
                P.op(ACT, lambda e: e.activation(out=sl_, in_=cf, func=AF.Sigmoid, scale=lngT[:, l, c:c + 1], bias=lnbT[:, l, c:c + 1]),
                     reads=[px + "cdwf%d" % c, "cwT"], writes=[px + "sgl%d" % ysl] + ([] if X["samp"] else ["cw_tok"]))
                P.op(DVE, lambda e: e.tensor_tensor(out=yb, in0=yb, in1=sl_, op=ALU.mult),
                     reads=[px + "ybf%d" % ysl, px + "sgl%d" % ysl], writes=[px + "ybf%d" % ysl])
                P.op(DVE, lambda e: e.tensor_tensor(out=cT[:, c, t0:t0 + n_], in0=yb, in1=X["sgc"][:, c, 0:n_], op=ALU.mult),
                     reads=[px + "ybf%d" % ysl, px + "sgc%d" % c], writes=["cT.%d" % X["k"], px + "cdwf%d" % c])

            c5 = Carver(CT_B + MT_B)
            wm = [c5.take([28, 128], BF16) for _ in range(2)]
            sm = [[c5.take([512], F32) for _ in range(2)] for _ in range(2)]
            m1 = [[c5.take([512], F32) for _ in range(2)] for _ in range(2)]
            p5ci = [0]

            def p5_load(mo):
                wsl = mo % 2
                if mo == 0:
                    P.op(POOL, lambda e: e.memset(small[:, 63:64], 0.0), writes=["wmfence"] + ["u%d" % c_ for c_ in range(6)] + wck)
                rd = ["wmfence"] if mo <= 1 else []
                P.op(POOL, lambda e: e.dma_start(
                    out=wm[wsl][:, 0:6, :], in_=w_ao[l, :, mo * 128:(mo + 1) * 128].rearrange("(k p) c -> p k c", p=128)),
                    reads=rd, writes=["wm%d.0" % wsl], dma="wm%d" % wsl)
                P.op(POOL, lambda e: e.dma_start(
                    out=wm[wsl][:, 6:12, :], in_=w_co[l, :, mo * 128:(mo + 1) * 128].rearrange("(k p) c -> p k c", p=128)),
                    reads=rd, writes=["wm%d.1" % wsl], dma="wm%d" % wsl)
                P.op(POOL, lambda e: e.dma_start(
                    out=wm[wsl][:, 12:20, :], in_=w_in[l, :, 5376 + mo * 128:5376 + (mo + 1) * 128].rearrange("(k p) c -> p k c", p=128)),
                    reads=rd, writes=["wm%d.2" % wsl], dma="wm%d" % wsl)
                P.op(POOL, lambda e: e.dma_start(
                    out=wm[wsl][:, 20:28, :], in_=w_in[l, :, 6400 + mo * 128:6400 + (mo + 1) * 128].rearrange("(k p) c -> p k c", p=128)),
                    reads=rd, writes=["wm%d.3" % wsl], dma="wm%d" % wsl)

            def p5_chunk(mo, k5):
                wsl = mo % 2
                t0, n_ = TCH[k5]
                wmk = ["wm%d.%d" % (wsl, i_) for i_ in range(4)]
                par = p5ci[0] % 2
                p5ci[0] += 1
                b0 = 4 * par

                def mm5(e):
                    for kc in range(6):
                        e.matmul(ps[:, b0, 0:n_], lhsT=wm[wsl][:, kc, :], rhs=aT[:, kc, t0:t0 + n_], start=(kc == 0), stop=(kc == 5))
                    for kc in range(6):
                        e.matmul(ps[:, b0 + 1, 0:n_], lhsT=wm[wsl][:, 6 + kc, :], rhs=cT[:, kc, t0:t0 + n_], start=(kc == 0), stop=(kc == 5))
                    for kc in range(8):
                        e.matmul(ps[:, b0 + 2, 0:n_], lhsT=wm[wsl][:, 12 + kc, :], rhs=hT[:, kc, t0:t0 + n_], start=(kc == 0), stop=(kc == 7))
                    for kc in range(8):
                        i = e.matmul(ps[:, b0 + 3, 0:n_], lhsT=wm[wsl][:, 20 + kc, :], rhs=hT[:, kc, t0:t0 + n_], start=(kc == 0), stop=(kc == 7))
                    return i
                bk = ["bank%d" % (b0 + i_) for i_ in range(4)]
                P.op(PE, mm5, reads=["hT", "aT", "cT.%d" % k5] + wmk, writes=bk)
                for i_ in range(2):
                    P.op(ACT, lambda e, i_=i_: e.activation(
                        out=sm[par][i_][:, 0:n_], in_=ps[:, b0 + 2 + i_, 0:n_], func=AF.Sigmoid),
                        reads=[bk[2 + i_]], writes=["sm%d%d" % (par, i_)])
                    P.op(DVE, lambda e, i_=i_: e.tensor_tensor(
                        out=m1[par][i_][:, 0:n_], in0=ps[:, b0 + i_, 0:n_], in1=sm[par][i_][:, 0:n_], op=ALU.mult),
                        reads=[bk[i_], "sm%d%d" % (par, i_)], writes=["m1%d%d" % (par, i_)])
                P.op(DVE, lambda e: e.tensor_tensor(
                    out=mT[:, mo, t0:t0 + n_], in0=m1[par][0][:, 0:n_], in1=m1[par][1][:, 0:n_], op=ALU.add),
                    reads=["m1%d0" % par, "m1%d1" % par], writes=["mT"] + (["diag0"] if mo == 0 else []))

            ctxs = [main_ctx(t0_) for (t0_, _n) in TCH[0:4]]
            prev = None
            for ci4, X in enumerate(ctxs):
                lastmain = (ci4 == 3)
                for c in range(6 + 1):
                    if c < 6:
                        projAB(X, c)
                    if c < 6 and prev is not None:
                        ln_apply(prev, c)
                    if c < 6 and lastmain:
                        projAB(s_ctx, c)
                    if c >= 1:
                        convc(X, c - 1)
                for c in range(6):
                    projG(X, c)
                    if lastmain:
                        projG(s_ctx, c)
                ln_head(X)
                prev = X
            def trul(e):
                for c in range(6):
                    i = e.transpose(out=ps[0:30, 0 + c // 4, (c % 4) * 128:(c % 4 + 1) * 128], in_=u_last[:, c, :], identity=ident_f[:, :])
                return i
            P.op(PE, trul, reads=["u_last%d" % c for c in range(6)], writes=["bank0", "bank1"])
            P.op(ACT, lambda e: e.copy(out=us_tok[0:30, :], in_=ps[0:30, 0:2, :].rearrange("p a b -> p (a b)")[:, 0:768]),
                 reads=["bank0", "bank1"], writes=["ustok"])
            P.op(SP, lambda e: e.dma_start(out=pconv[l], in_=us_tok[0:30, :]), reads=["ustok"], dma="pconv")
            p5_load(0)
            p5_load(1)
            ln_head(s_ctx)
            for c in range(6):
                ln_apply(prev, c)
                ln_apply(s_ctx, c)
                if c < 3:
                    p5_chunk(0, c)

            def trus(e):
                for c in range(6):
                    i = e.transpose(out=ps[0:NS, 2 + c // 4, (c % 4) * 128:(c % 4 + 1) * 128], in_=u_s[:, c, :], identity=ident_f[:, :])
                return i
            P.op(PE, trus, reads=["u_s%d" % c for c in range(6)], writes=["bank2", "bank3"])
            P.op(ACT, lambda e: e.copy(out=us_tok[0:NS, :], in_=ps[0:NS, 2:4, :].rearrange("p a b -> p (a b)")[:, 0:768]),
                 reads=["bank2", "bank3"], writes=["ustok"])
            P.op(SP, lambda e: e.dma_start(out=sconv[l, :, 29, :], in_=us_tok[0:NS, :]), reads=["ustok"], dma="usout")
            p5_chunk(0, 3)
            p5_chunk(0, 4)
            P.mark(5)
            P.barrier()
            xsrc = x_p if l == 0 else y_p
            for half in range(2):
                P.op(POOL, lambda e, half=half: e.dma_start(
                    out=wo_sb[:, :, half * 512:(half + 1) * 512],
                    in_=w_o[l, :, half * 512:(half + 1) * 512].rearrange("(k p) c -> p k c", p=128)),
                    writes=["W:wo%d" % half], dma="wo", nobar=True)
            P.op(SP, lambda e: e.dma_start(out=gpost[:], in_=n_post[l].partition_broadcast(128)), writes=["W:gpost"], dma="W:gpost", nobar=True)
            if not last:
                P.op(SP, lambda e: _ncdma(e, gpreT[:], n_pre[l + 1].rearrange("(k p) -> p k", p=128)), writes=["W:gpre"], dma="W:gpre", nobar=True)
            for T in range(3):
                P.op(SP, lambda e, T=T: e.dma_start(out=xt[T][:], in_=xsrc[T * 128:(T + 1) * 128, :]),
                     writes=["W:xt%d" % T], dma="W:xt%d" % T, nobar=True)
            for mo in range(1, 8):
                if mo >= 2:
                    p5_load(mo)
                for k5 in range(5):
                    p5_chunk(mo, k5)
            P.barrier()
            P.mark(6)
            if not last:
                prefetch_qkv(l + 1)
            xsrc = x_p if l == 0 else y_p
            PAIRS = ((0, 1), (2, 3), (6, 7))

            def p6_info(T):
                samp = (T == 16)
                npart = NS if samp else 128
                if samp:
                    return samp, npart, xs_t[:, :], "xs_t", S
                return samp, npart, xt[T % 3][:, :], "W:xt%d" % (T % 3), T * 128

            def p6_M(T):
                samp, npart, xcur, xkey, col0 = p6_info(T)
                if not samp and T >= 3:
                    sl = T % 3
                    P.op(SP, lambda e: e.dma_start(out=xt[sl][:], in_=xsrc[T * 128:(T + 1) * 128, :]),
                         reads=["ydram%d" % T], writes=[xkey], dma="W:xt%d" % sl)
                b0, b1 = PAIRS[T % 3]

                def mm6(e):
                    for half, bb in enumerate((b0, b1)):
                        for kc in range(8):
                            i = e.matmul(ps[0:npart, bb, :], lhsT=mT[:, kc, col0:col0 + npart],
                                         rhs=wo_sb[:, kc, half * 512:(half + 1) * 512], start=(kc == 0), stop=(kc == 7))
                    return i
                P.op(PE, mm6, reads=["mT", "W:wo0", "W:wo1"], writes=["bank%d" % b0, "bank%d" % b1])

            def p6_E(T):
                samp, npart, xcur, xkey, col0 = p6_info(T)
                b0, b1 = PAIRS[T % 3]
                par = T % 2
                ob = ps[0:npart, b0:b1 + 1, :].rearrange("p a b -> p (a b)")
                q3 = T % 3
                ss = small[0:npart, 40 + q3 * 4:40 + q3 * 4 + 1]
                sd = small[0:npart, 40 + q3 * 4 + 1:40 + q3 * 4 + 2]
                rs = small[0:npart, 40 + q3 * 4 + 2:40 + q3 * 4 + 3]
                kq = "q%d" % q3
                bks = ["bank%d" % b0, "bank%d" % b1]
                P.op(ACT, lambda e: e.activation(out=p1_junk[0:npart, :], in_=ob, func=AF.Square, accum_out=ss),
                     reads=bks, writes=["p1junk", kq + "ss"])
                P.op(ACT, lambda e: e.activation(out=sd, in_=ss, func=AF.Sqrt, scale=1.0 / DM, bias=eps_rms[0:npart, :]),
                     reads=[kq + "ss"], writes=[kq + "sd"])
                P.op(DVE, lambda e: e.reciprocal(out=rs, in_=sd), reads=[kq + "sd"], writes=[kq + "rs"])
                P.op(DVE, lambda e: e.tensor_tensor(out=p6_t[par][0:npart, :], in0=ob, in1=gpost[0:npart, :], op=ALU.mult),
                     reads=bks + ["W:gpost"], writes=["p6t"])
                P.op(DVE, lambda e: e.scalar_tensor_tensor(
                    out=xcur, in0=p6_t[par][0:npart, :], scalar=rs, in1=xcur, op0=ALU.mult, op1=ALU.add),
                    reads=["p6t", kq + "rs", xkey], writes=[xkey])
                if not samp:
                    sl = T % 3
                    P.op(SP, lambda e: e.dma_start(out=y_p[T * 128:(T + 1) * 128, :], in_=xt[sl][:]),
                         reads=[xkey], writes=["ydram%d" % T], dma="xo%d" % sl)
                elif last:
                    P.op(SP, lambda e: e.dma_start(out=y_s, in_=xs_t[:, :]), reads=["xs_t"], dma="ys")
                if not last:
                    p1_norm(xcur, npart, xkey, T % 3)

            def p6_R(T):
                samp, npart, xcur, xkey, col0 = p6_info(T)
                if not last:
                    p1_tr(npart, col0, T % 3, T % 2)

            for T in range(17 + 2):
                if T < 17:
                    p6_M(T)
                if T >= 2:
                    p6_R(T - 2)
                if T < 17:
                    p6_E(T)
            P.barrier()

        for l_ in range(nlay):
            layer(l_)
        P.build()
        stats = P.stats
    return nc, stats


_CACHE = {}


def _consts():
    ident = np.eye(128, dtype=np.float32)
    k = np.arange(128)[:, None]
    q = np.arange(128)[None, :]
    mask = np.stack([(k <= q), (k >= q)], axis=1).astype(np.float32)
    inv = np.power(np.float32(500000.0), -np.arange(0, 16, 2, dtype=np.float32) / np.float32(16)).astype(np.float32)
    rope = np.zeros((128, 2, 48, 8), np.float32)
    p = np.arange(128)
    for g in range(3):
        for T in range(16):
            base, stride = tile_cols(g, T)
            pos = (base + stride * p).astype(np.float32)
            ang = (pos[:, None] * inv[None, :]).astype(np.float32)
            rope[:, 0, g * 16 + T, :] = np.cos(ang)
            rope[:, 1, g * 16 + T, :] = np.sin(ang)
    angs = (np.float32(8192.0) * inv).astype(np.float32)
    ropes = np.zeros((NS, 2, 8), np.float32)
    ropes[:, 0, :] = np.cos(angs)[None, :]
    ropes[:, 1, :] = np.sin(angs)[None, :]
    sel = np.zeros((NS, NS, 128), np.float32)
    for b in range(NS):
        sel[b, b, :] = 1.0
    return dict(c_ident=ident, c_mask=mask, c_rope=rope, c_ropes=ropes, c_sel=sel)


def kernel(x_prompt, x_sample, cache_k0, cache_v0, cache_k1, cache_v1, cache_k2, cache_v2,
           state_conv, w_in, w_attn_out, w_conv_out, w_out, conv_w, conv_ln_g, conv_ln_b,
           norm_pre, norm_post):
    ncores = 8
    if "nc" not in _CACHE:
        _CACHE["nc"] = build_program(NL)[0]
    nc = _CACHE["nc"]
    f = lambda a: np.ascontiguousarray(np.asarray(a, dtype=np.float32))
    consts = _consts()
    caches_k = [f(cache_k0), f(cache_k1), f(cache_k2)]
    caches_v = [f(cache_v0), f(cache_v1), f(cache_v2)]
    shared = dict(w_in=f(w_in), w_ao=f(w_attn_out), w_co=f(w_conv_out), w_o=f(w_out), conv_w=f(conv_w),
                  ln_g=f(conv_ln_g), ln_b=f(conv_ln_b), n_pre=f(norm_pre), n_post=f(norm_post), **consts)
    xp = f(x_prompt)
    xs = f(x_sample)
    sc = f(state_conv)
    in_maps = []
    for c in range(ncores):
        m = dict(shared)
        m["x_p"] = xp[c]
        m["x_s"] = np.ascontiguousarray(xs[NS * c:NS * (c + 1), 0, :])
        for g in range(3):
            W = GROUPS[g][0]
            m["ck%d" % g] = np.ascontiguousarray(caches_k[g][:, NS * c:NS * (c + 1)].reshape(NL, NS, W, 256))
            m["cv%d" % g] = np.ascontiguousarray(caches_v[g][:, NS * c:NS * (c + 1)].reshape(NL, NS, W, 256))
        m["stc"] = np.ascontiguousarray(sc[:, NS * c:NS * (c + 1)])
        in_maps.append(m)
    res = run_bass_kernel_spmd(nc, in_maps, core_ids=list(range(ncores)))
    R = res.results
    yp = np.stack([R[c]["y_p"] for c in range(ncores)], axis=0)
    ys = np.concatenate([R[c]["y_s"] for c in range(ncores)], axis=0)[:, None, :]
    outs = [yp, ys]
    for g in range(3):
        W = GROUPS[g][0]
        for nm in ("pk", "pv"):
            outs.append(np.stack([R[c]["%s%d" % (nm, g)] for c in range(ncores)], axis=1).reshape(NL, ncores, W, 4, 64))
    outs.append(np.stack([R[c]["pconv"] for c in range(ncores)], axis=1))
    for g in range(3):
        W = GROUPS[g][0]
        for nm in ("sk", "sv"):
            outs.append(np.concatenate([R[c]["%s%d" % (nm, g)] for c in range(ncores)], axis=1).reshape(NL, NS * ncores, W, 4, 64))
    outs.append(np.concatenate([R[c]["sconv"] for c in range(ncores)], axis=1))
    return tuple(np.ascontiguousarray(o, dtype=np.float32) for o in outs)
```

```python
import numpy as np
import concourse.bass as bass
import concourse.mybir as mybir
from concourse.alu_op_type import AluOpType as ALU
from concourse.bass_utils import run_bass_kernel_spmd
from contextlib import ExitStack

F32 = mybir.dt.float32
BF16 = mybir.dt.bfloat16
AF = mybir.ActivationFunctionType
AX = mybir.AxisListType

PE, ACT, DVE, POOL, SP = "tensor", "scalar", "vector", "gpsimd", "sync"

NL = 4
S = 2048
NS = 4
TOK = S + NS
DM = 1024
NIN = 7424
GROUPS = ((128, 1), (512, 4), (2048, 16))
RMS_EPS = 1e-6
LN_EPS = 1e-5


class Prog:
    def __init__(self, nc, es):
        self.nc = nc
        self.es = es
        self.ops = []

    def op(self, eng, emit, reads=(), writes=(), dma=None, nobar=False):
        if getattr(self, "disabled", False):
            return
        bk = [k for k in reads if k.startswith("bank")]
        if bk:
            reads = [k for k in reads if not k.startswith("bank")]
            writes = list(writes) + bk
        self.ops.append(dict(eng=eng, emit=emit, reads=tuple(reads), writes=tuple(writes), dma=dma, bar=False, nobar=nobar))

    def barrier(self):
        self.ops.append(dict(bar=True))

    def mark(self, n):
        import os
        lim = float(os.environ.get("KSTOP", "99"))
        self.disabled = n > lim

    def build(self):
        nc, es = self.nc, self.es
        ops = self.ops
        engs = (PE, ACT, DVE, POOL, SP)
        last_w = {}
        readers = {}
        n = len(ops)
        deps = [set() for _ in range(n)]
        last_on_eng = {}
        all_dma_since = set()
        bar_deps = {e: set() for e in engs}
        for i, o in enumerate(ops):
            if o["bar"]:
                snap = set(last_on_eng.values()) | all_dma_since
                for e in engs:
                    bar_deps[e] |= snap
                last_w = {k: v for k, v in last_w.items() if k.startswith("W:")}
                readers = {k: v for k, v in readers.items() if k.startswith("W:")}
                all_dma_since = set()
                continue
            d = set()
            for k in o["reads"]:
                if k in last_w:
                    d.add(last_w[k])
            for k in o["writes"]:
                if k in last_w:
                    d.add(last_w[k])
                for r in readers.get(k, ()):
                    d.add(r)
            d.discard(i)
            if o["eng"] == PE and o["dma"] is None:
                d = {j for j in d if not (ops[j]["eng"] == PE and ops[j]["dma"] is None)}
            if bar_deps[o["eng"]]:
                d |= bar_deps[o["eng"]]
                bar_deps[o["eng"]] = set()
            deps[i] = d
            for k in o["reads"]:
                readers.setdefault(k, []).append(i)
            for k in o["writes"]:
                last_w[k] = i
                readers[k] = []
            if o["dma"] is not None:
                if (o["reads"] or o["writes"]) and not o["nobar"]:
                    all_dma_since.add(i)
            else:
                last_on_eng[o["eng"]] = i
        needed = set()
        for d in deps:
            needed |= d
        sems = {}
        sig = [None] * n
        cnt = {}
        for i, o in enumerate(ops):
            if o["bar"]:
                continue
            if o["dma"] is not None:
                sn = "d_" + o["dma"].replace("W:", "").replace(".", "_")
                cnt[sn] = cnt.get(sn, 0) + 16
                sig[i] = (sn, cnt[sn], 16)
            elif i in needed:
                sn = "e_" + o["eng"]
                cnt[sn] = cnt.get(sn, 0) + 1
                sig[i] = (sn, cnt[sn], 1)
        per_eng = {e: [] for e in engs}
        waited = {e: {} for e in engs}
        nwaits = 0
        for i, o in enumerate(ops):
            if o["bar"]:
                continue
            e = o["eng"]
            ws = {}
            for j in deps[i]:
                sn, val, _ = sig[j]
                if o["dma"] is None and ops[j]["dma"] is None and ops[j]["eng"] == e == PE:
                    continue
                if waited[e].get(sn, 0) >= val:
                    continue
                ws[sn] = max(ws.get(sn, 0), val)
            for sn, val in ws.items():
                waited[e][sn] = val
                nwaits += 1
            per_eng[e].append((list(ws.items()), o["emit"], sig[i]))
        finals = [(sn, v) for sn, v in cnt.items()]
        self.stats = dict(nops=n, nwaits=nwaits, nsems=len(cnt),
                          per_eng={e: len(per_eng[e]) for e in engs})
        for sn in cnt:
            sems[sn] = es.enter_context(nc.semaphore("s_" + sn))
        block = es.enter_context(nc.Block())

        def make(ename):
            lst = per_eng[ename]

            def body(eng):
                for ws, emit, sg in lst:
                    for sn, val in ws:
                        eng.wait_ge(sems[sn], val)
                    ins = emit(eng)
                    if sg is not None:
                        ins.then_inc(sems[sg[0]], sg[2])
                if ename == SP:
                    for sn, v in finals:
                        eng.wait_ge(sems[sn], v)
            return body

        block.sync(make(SP))
        block.tensor(make(PE))
        block.scalar(make(ACT))
        block.vector(make(DVE))
        block.gpsimd(make(POOL))


def tile_cols(g, T):
    if g == 0:
        return 128 * T, 1
    if g == 1:
        return 512 * (T % 4) + (T // 4), 4
    return T, 16


def has_prev(g, T):
    if g == 0:
        return T >= 1
    if g == 1:
        return (T % 4) >= 1
    return False


def build_program(nlay=NL):
    nc = bass.Bass("TRN2", target_bir_lowering=False)
    din = lambda name, shape: nc.dram_tensor(name, list(shape), F32, kind="ExternalInput").ap()
    dout = lambda name, shape: nc.dram_tensor(name, list(shape), F32, kind="ExternalOutput").ap()
    x_p = din("x_p", [S, DM])
    x_s = din("x_s", [NS, DM])
    ck = [din("ck%d" % g, [NL, NS, GROUPS[g][0], 256]) for g in range(3)]
    cv = [din("cv%d" % g, [NL, NS, GROUPS[g][0], 256]) for g in range(3)]
    stc = din("stc", [NL, NS, 30, 768])
    w_in = din("w_in", [NL, DM, NIN])
    w_ao = din("w_ao", [NL, 768, DM])
    w_co = din("w_co", [NL, 768, DM])
    w_o = din("w_o", [NL, DM, DM])
    conv_w = din("conv_w", [NL, 31, 768])
    ln_g = din("ln_g", [NL, 768])
    ln_b = din("ln_b", [NL, 768])
    n_pre = din("n_pre", [NL, DM])
    n_post = din("n_post", [NL, DM])
    c_ident = din("c_ident", [128, 128])
    c_mask = din("c_mask", [128, 2, 128])
    c_rope = din("c_rope", [128, 2, 48, 8])
    c_ropes = din("c_ropes", [NS, 2, 8])
    c_sel = din("c_sel", [NS, NS, 128])

    y_p = dout("y_p", [S, DM])
    y_s = dout("y_s", [NS, DM])
    pk = [dout("pk%d" % g, [NL, GROUPS[g][0], 256]) for g in range(3)]
    pv = [dout("pv%d" % g, [NL, GROUPS[g][0], 256]) for g in range(3)]
    pconv = dout("pconv", [NL, 30, 768])
    sk = [dout("sk%d" % g, [NL, NS, GROUPS[g][0], 256]) for g in range(3)]
    sv = [dout("sv%d" % g, [NL, NS, GROUPS[g][0], 256]) for g in range(3)]
    sconv = dout("sconv", [NL, NS, 30, 768])

    es = ExitStack()
    with es:
        P = Prog(nc, es)
        sb = lambda name, shape, dt: es.enter_context(nc.sbuf_tensor(name, list(shape), dt))
        ps = es.enter_context(nc.psum_tensor("ps", [128, 8, 512], F32))

        def bank(i):
            return ps[:, i, :]

        hT = sb("hT", [128, 8, TOK], BF16)
        aT = sb("aT", [128, 6, TOK], BF16)
        ident_f = sb("ident_f", [128, 128], F32)
        ident_b = sb("ident_b", [128, 128], BF16)
        ones_b = sb("ones_b", [128, 128], BF16)
        ones_f = sb("ones_f", [128, 4], F32)
        mask_b = sb("mask_b", [128, 2, 128], BF16)
        rope = sb("rope", [128, 2, 48, 8], F32)
        ropes = sb("ropes", [NS, 2, 8], F32)
        sel_b = sb("sel_b", [NS, NS, 128], BF16)
        gpre = sb("gpre", [128, DM], F32)
        gpost = sb("gpost", [128, DM], F32)
        lngT = sb("lngT", [128, NL, 6], F32)
        lnbT = sb("lnbT", [128, NL, 6], F32)
        xs_t = sb("xs_t", [NS, DM], F32)
        sga_s = sb("sga_s", [128, 6, NS], F32)
        small = sb("small", [128, 64], F32)
        u_s = sb("u_s", [128, 6, NS], F32)
        s_cdwf = sb("s_cdwf", [128, 6, NS], F32)
        s_cdwb = sb("s_cdwb", [128, 6, NS], BF16)
        s_cdw2b = sb("s_cdw2b", [128, 6, NS], BF16)
        s_sgc = sb("s_sgc", [128, 6, NS], BF16)
        s_sgt = [sb("s_sgt%d" % i_, [128, NS], F32) for i_ in range(2)]
        s_mean = sb("s_mean", [128, NS], F32)
        s_vv = sb("s_vv", [128, NS], F32)
        s_ybf = [sb("s_ybf%d" % i_, [128, NS], BF16) for i_ in range(2)]
        s_sgl = [sb("s_sgl%d" % i_, [128, NS], BF16) for i_ in range(2)]
        us_tok = sb("us_tok", [128, 768], F32)
        ARENA_B = 131072
        arena = sb("arena", [128, ARENA_B // 4], F32)

        class Carver:
            def __init__(self, start):
                self.off = start

            def take(self, shape, dt):
                esz = 4 if dt == F32 else 2
                nel = int(np.prod(shape))
                nb = (nel * esz + 31) // 32 * 32
                assert self.off % 4 == 0
                assert self.off + nb <= ARENA_B, ("arena overflow", self.off + nb)
                v = arena[:, self.off // 4:(self.off + nb) // 4]
                if dt != F32:
                    v = v.bitcast(dt)
                v = v[:, 0:nel]
                self.off += nb
                if len(shape) == 2:
                    return v.rearrange("p (a b) -> p a b", a=shape[0])
                if len(shape) == 3:
                    return v.rearrange("p (a b c) -> p a b c", a=shape[0], b=shape[1])
                return v

        ln_tok = Carver(0).take([2, 768], F32)
        CT_B = (6 * TOK * 2 + 31) // 32 * 32
        MT_B = (8 * TOK * 2 + 31) // 32 * 32
        cT = Carver(0).take([6, TOK], BF16)
        mT = Carver(CT_B).take([8, TOK], BF16)
        assert CT_B % 32 == 0 and MT_B % 32 == 0

        P.op(SP, lambda e: e.dma_start(out=ident_f[:], in_=c_ident), writes=["c0"], dma="c")
        P.op(SP, lambda e: e.dma_start(out=rope[:], in_=c_rope), writes=["c1"], dma="c")
        P.op(SP, lambda e: e.dma_start(out=ropes[:], in_=c_ropes), writes=["c2"], dma="c")
        P.op(POOL, lambda e: e.dma_start(out=sel_b[:], in_=c_sel), writes=["c3"], dma="c")
        P.op(SP, lambda e: e.dma_start(out=xs_t[:], in_=x_s), writes=["c4"], dma="c")
        P.op(POOL, lambda e: e.dma_start(out=mask_b[:], in_=c_mask), writes=["c5"], dma="c")
        P.op(SP, lambda e: e.dma_start(out=ln_tok[0:NL, 0, :], in_=ln_g), writes=["c6"], dma="c")
        P.op(SP, lambda e: e.dma_start(out=ln_tok[0:NL, 1, :], in_=ln_b), writes=["c7"], dma="c")
        P.op(SP, lambda e: e.dma_start(out=gpre[:], in_=n_pre[0].partition_broadcast(128)), writes=["W:gpre"], dma="W:gpre")
        P.op(DVE, lambda e: e.memset(ones_b[:], 1.0), writes=["ones_b"])
        P.op(DVE, lambda e: e.memset(ones_f[:], 1.0), writes=["ones_f"])
        P.barrier()
        P.op(DVE, lambda e: e.tensor_copy(out=ident_b[:], in_=ident_f[:]), writes=["ident_b"])

        def trln(e):
            for a in range(2):
                for c in range(6):
                    i = e.transpose(out=ps[:, 7, (a * 6 + c) * 4:(a * 6 + c) * 4 + NL], in_=ln_tok[0:NL, a, c * 128:(c + 1) * 128],
                                    identity=ident_f[0:NL, 0:NL])
            return i
        P.op(PE, trln, writes=["bank7"])
        P.op(ACT, lambda e: e.copy(out=lngT[:, :, :].rearrange("p l c -> p c l"), in_=ps[:, 7, 0:24].rearrange("p (c l) -> p c l", c=6)),
             reads=["bank7"], writes=["lngT"])
        P.op(ACT, lambda e: e.copy(out=lnbT[:, :, :].rearrange("p l c -> p c l"), in_=ps[:, 7, 24:48].rearrange("p (c l) -> p c l", c=6)),
             reads=["bank7"], writes=["lnbT"])
        P.barrier()

        P.mark(1)
        def p1_norm(xt_ap, npart, xkey, slot):
            ss = small[0:npart, slot * 4 + 0:slot * 4 + 1]
            sd = small[0:npart, slot * 4 + 1:slot * 4 + 2]
            rs = small[0:npart, slot * 4 + 2:slot * 4 + 3]
            hs = p1_hs[slot]
            ks = "p1ss%d" % slot
            P.op(ACT, lambda e: e.activation(out=p1_junk[0:npart, :], in_=xt_ap, func=AF.Square, accum_out=ss),
                 reads=[xkey], writes=["p1junk", ks])
            P.op(ACT, lambda e: e.activation(out=sd, in_=ss, func=AF.Sqrt, scale=1.0 / DM, bias=eps_rms[0:npart, :]),
                 reads=[ks], writes=[ks + "d"])
            P.op(DVE, lambda e: e.reciprocal(out=rs, in_=sd), reads=[ks + "d"], writes=[ks + "r"])
            P.op(DVE, lambda e: e.scalar_tensor_tensor(out=hs[0:npart, :], in0=xt_ap, scalar=rs, in1=gpre[0:npart, :],
                                                       op0=ALU.mult, op1=ALU.mult),
                 reads=[xkey, ks + "r", "W:gpre"], writes=["hs%d" % slot])

        def p1_tr(npart, col0, slot, bslot):
            hs = p1_hs[slot]
            trb = bank(4 + bslot).bitcast(BF16)

            def tr(e):
                for kc in range(8):
                    i = e.transpose(out=trb[:, kc * 128:kc * 128 + npart], in_=hs[0:npart, kc * 128:(kc + 1) * 128],
                                    identity=ident_b[0:npart, 0:npart])
                return i
            P.op(PE, tr, reads=["hs%d" % slot], writes=["bank%d" % (4 + bslot)])
            trv = trb.rearrange("p (k t) -> p k t", k=8)
            P.op(ACT, lambda e: e.copy(out=hT[:, :, col0:col0 + npart], in_=trv[:, :, 0:npart]),
                 reads=["bank%d" % (4 + bslot)], writes=["hT"])

        eps_rms = sb("eps_rms", [128, 1], F32)
        eps_ln = sb("eps_ln", [128, 1], F32)
        P.op(DVE, lambda e: e.memset(eps_rms[:], RMS_EPS), writes=["eps_rms"])
        P.op(DVE, lambda e: e.memset(eps_ln[:], LN_EPS), writes=["eps_ln"])
        P.barrier()

        c6 = Carver(88192)
        xt = [c6.take([DM], F32) for _ in range(3)]
        p1_hs = [c6.take([DM], BF16) for _ in range(3)]
        p1_junk = c6.take([DM], BF16)
        p6_t1 = c6.take([DM], F32)
        p6_t = [p6_t1, p6_t1]
        assert c6.off <= 114688
        wo_sb = Carver(114688).take([8, DM], BF16)
        cpre = Carver(0)
        wqkv_pre = [cpre.take([8, 384], BF16) for _ in range(2)]
        wga_pre = [cpre.take([8, 128], BF16) for _ in range(3)]

        import os as _os
        NOPF = bool(_os.environ.get("KNOPF"))

        def prefetch_qkv(ln):
            if NOPF:
                return
            for part, coff in enumerate((0, 768, 1536)):
                P.op(POOL, lambda e, part=part, coff=coff: e.dma_start(
                    out=wqkv_pre[0][:, :, part * 128:(part + 1) * 128],
                    in_=w_in[ln, :, coff:coff + 128].rearrange("(k p) c -> p k c", p=128)),
                    writes=["W:wqkv%d.%d" % (0, part)], dma="W:wqkv%d" % 0, nobar=True)
            P.op(POOL, lambda e: e.dma_start(
                out=wga_pre[0][:], in_=w_in[ln, :, 2304:2304 + 128].rearrange("(k p) c -> p k c", p=128)),
                writes=["W:wga%d" % 0], dma="W:wga%d" % 0, nobar=True)

        prefetch_qkv(0)
        for T in range(16):
            sl = T % 3
            P.op(SP, lambda e, T=T, sl=sl: e.dma_start(out=xt[sl][:], in_=x_p[T * 128:(T + 1) * 128, :]),
                 writes=["W:xt%d" % sl], dma="W:xt%d" % sl)
            p1_norm(xt[sl][:], 128, "W:xt%d" % sl, T % 3)
            if T >= 1:
                p1_tr(128, (T - 1) * 128, (T - 1) % 3, (T - 1) % 2)
        p1_norm(xs_t[:], NS, "xs_t", 16 % 3)
        p1_tr(128, 15 * 128, 15 % 3, 15 % 2)
        p1_tr(NS, S, 16 % 3, 0)
        P.barrier()

        TCH = [(0, 512), (512, 512), (1024, 512), (1536, 512), (S, NS)]

        def emit_copies(l, gate=()):
            for g in range(3):
                W = GROUPS[g][0]
                for src, dst in ((ck[g], sk[g]), (cv[g], sv[g])):
                    for b in range(NS):
                        P.op(SP, lambda e, src=src, dst=dst, W=W, b=b: e.dma_start(
                            out=dst[l, b, 0:W - 1, :].rearrange("w c -> (w c)").rearrange("(s e) -> s e", s=16),
                            in_=src[l, b, 1:W, :].rearrange("w c -> (w c)").rearrange("(s e) -> s e", s=16)), reads=gate, dma="cp", nobar=True)
            for b in range(NS):
                P.op(SP, lambda e, b=b: e.dma_start(
                    out=sconv[l, b, 0:29, :].rearrange("w c -> (w c)").rearrange("(s e) -> s e", s=16),
                    in_=stc[l, b, 1:30, :].rearrange("w c -> (w c)").rearrange("(s e) -> s e", s=16)), dma="cp")

        def layer(l):
            last = (l == nlay - 1)
            P.mark(2)
            c2 = Carver(0)
            qb = c2.take([NS, 768], F32)
            Pb = c2.take([NS, 768], F32)
            p3_small = c2.take([512], F32)
            pnx = c2.take([3, 768], F32)
            PVb = c2.take([NS, 768], BF16)
            qs_bf = c2.take([6, 128], BF16)
            p3_end = c2.off
            ckv = Carver(105472)
            Kc = ckv.take([NS, 768], F32)
            Vc = ckv.take([NS, 768], F32)
            c2 = Carver(0)
            wqkv = [c2.take([8, 384], BF16) for _ in range(2)]
            wga = [c2.take([8, 128], BF16) for _ in range(3)]
            stage = [c2.take([384], F32) for _ in range(3)]
            rt = [c2.take([4, 4, 8], F32) for _ in range(3)]
            QK = [c2.take([2, S], BF16) for _ in range(2)]
            Vb = [c2.take([16, 128], BF16) for _ in range(2)]
            OT = c2.take([3, S], F32)
            Dn = c2.take([S], F32)
            rD = c2.take([S], F32)
            pt = [[c2.take([256], BF16) for _ in range(2)] for _ in range(2)]
            sga = [c2.take([512], BF16) for _ in range(2)]
            tt1 = c2.take([512], F32)
            tt = [tt1, tt1]
            c2.off = max(c2.off, p3_end)
            qkvs = c2.take([6, 384], F32)
            assert c2.off <= 105472, c2.off
            WCONV_OFF = 59392
            assert p3_end <= WCONV_OFF and c2.off - 9216 >= WCONV_OFF + 36864, (p3_end, c2.off)
            wconv = Carver(WCONV_OFF).take([8, 2304], BF16)

            for g in range(3):
                dil = GROUPS[g][1]
                L = GROUPS[g][0]
                P.op(SP, lambda e, g=g, dil=dil, L=L: e.dma_start(
                    out=Kc[:, :, g * 256:(g + 1) * 256], in_=ck[g][l, :, 0:L:dil, :].rearrange("b p c -> p b c")),
                    writes=["W:Kc%d" % g], dma="kc", nobar=True)
                P.op(SP, lambda e, g=g, dil=dil, L=L: e.dma_start(
                    out=Vc[:, :, g * 256:(g + 1) * 256], in_=cv[g][l, :, 0:L:dil, :].rearrange("b p c -> p b c")),
                    writes=["W:Vc%d" % g], dma="vc", nobar=True)
            it = 0
            for sp in range(2):
                for g in range(3):
                    P.mark(2.0)
                    gi = 2 * g + sp
                    wsl = it % 2
                    qsl = it % 2
                    it += 1
                    c0 = g * 256 + sp * 128
                    for part, coff in enumerate((c0, 768 + c0, 1536 + c0)):
                        if it == 1 and not NOPF:
                            break
                        P.op(POOL, lambda e, wsl=wsl, part=part, coff=coff: e.dma_start(
                            out=wqkv[wsl][:, :, part * 128:(part + 1) * 128],
                            in_=w_in[l, :, coff:coff + 128].rearrange("(k p) c -> p k c", p=128)),
                            writes=["W:wqkv%d.%d" % (wsl, part)], dma="wqkv%d" % wsl)
                    if it != 1 or NOPF:
                        P.op(POOL, lambda e, g=g, gi=gi: e.dma_start(
                            out=wga[g][:], in_=w_in[l, :, 2304 + gi * 128:2304 + (gi + 1) * 128].rearrange("(k p) c -> p k c", p=128)),
                            writes=["W:wga%d" % g], dma="W:wga%d" % g)
                    wkeys = ["W:wqkv%d.%d" % (wsl, p_) for p_ in range(3)]
                    def stage_AB(T, g=g, sp=sp, gi=gi, wsl=wsl, wkeys=wkeys):
                        npart = 128 if T < 16 else NS
                        ssl = T % 3
                        bq = T % 2
                        if T < 16:
                            base, stride = tile_cols(g, T)
                            lcols = lambda kc: hT[:, kc, base:base + 127 * stride + 1:stride]
                        else:
                            base, stride = 0, 1
                            lcols = lambda kc: hT[:, kc, S:S + NS]
                        P.mark(2.1 if T < 16 else 2.15)

                        def mmq(e):
                            for kc in range(8):
                                i = e.matmul(ps[0:npart, bq, 0:384], lhsT=lcols(kc), rhs=wqkv[wsl][:, kc, :],
                                             start=(kc == 0), stop=(kc == 7))
                            return i
                        P.op(PE, mmq, reads=["hT"] + wkeys, writes=["bank%d" % bq])
                        if T < 16:
                            dst = stage[ssl][:, :]
                            skey = "stage%d" % ssl
                        else:
                            dst = qkvs[0:NS, gi, :]
                            skey = "qkvs%d" % gi
                        P.op(ACT, lambda e: e.copy(out=dst, in_=ps[0:npart, bq, 0:384]),
                             reads=["bank%d" % bq], writes=[skey])
                        qk4 = dst[:, 0:256].rearrange("p (a d) -> p a d", a=4)
                        x1 = qk4[:, :, 0:8]
                        x2 = qk4[:, :, 8:16]
                        if T < 16:
                            cosv = rope[:, 0, g * 16 + T:g * 16 + T + 1, :].to_broadcast([128, 4, 8])
                            sinv = rope[:, 1, g * 16 + T:g * 16 + T + 1, :].to_broadcast([128, 4, 8])
                        else:
                            cosv = ropes[:, 0:1, :].to_broadcast([NS, 4, 8])
                            sinv = ropes[:, 1:2, :].to_broadcast([NS, 4, 8])
                        r = rt[ssl]
                        rk = "rt%d" % ssl
                        for ti, (a_, b_) in enumerate(((x1, cosv), (x2, sinv), (x2, cosv), (x1, sinv))):
                            P.op(DVE, lambda e, ti=ti, a_=a_, b_=b_: e.tensor_tensor(
                                out=r[0:npart, ti, :, :], in0=a_, in1=b_, op=ALU.mult),
                                reads=[skey], writes=[rk + ".%d" % ti])
                        P.op(DVE, lambda e: e.tensor_tensor(
                            out=x1, in0=r[0:npart, 0, :, :], in1=r[0:npart, 1, :, :], op=ALU.subtract),
                            reads=[rk + ".0", rk + ".1", rk + ".2", rk + ".3"], writes=[skey])
                        P.op(DVE, lambda e: e.tensor_tensor(
                            out=x2, in0=r[0:npart, 2, :, :], in1=r[0:npart, 3, :, :], op=ALU.add),
                            reads=[rk + ".2", rk + ".3", skey], writes=[skey, rk + ".0", rk + ".1", rk + ".2", rk + ".3"])
                        if T == 16:
                            return
                        P.mark(2.2)
                        W, dil = GROUPS[g]
                        if (g == 0 and T == 15) or (g == 1 and T % 4 == 3) or g == 2:
                            row0 = base - (S - W)
                            P.op(SP, lambda e: e.dma_start(
                                out=pk[g][l, row0:row0 + 127 * stride + 1:stride, sp * 128:(sp + 1) * 128],
                                in_=stage[ssl][:, 128:256]), reads=[skey], dma="pkv%d" % ssl)
                            P.op(SP, lambda e: e.dma_start(
                                out=pv[g][l, row0:row0 + 127 * stride + 1:stride, sp * 128:(sp + 1) * 128],
                                in_=stage[ssl][:, 256:384]), reads=[skey], dma="pkv%d" % ssl)

                    def stage_C(T, qsl=qsl):
                        ssl = T % 3
                        bt = 2 + T % 2
                        skey = "stage%d" % ssl
                        P.mark(2.3)

                        def trq(e):
                            e.transpose(out=ps[:, bt, 0:128], in_=stage[ssl][:, 0:128], identity=ident_f[:])
                            return e.transpose(out=ps[:, bt, 128:256], in_=stage[ssl][:, 128:256], identity=ident_f[:])
                        P.op(PE, trq, reads=[skey], writes=["bank%d" % bt])
                        P.op(ACT, lambda e: e.copy(
                            out=QK[qsl][:, :, T * 128:(T + 1) * 128],
                            in_=ps[:, bt, 0:256].rearrange("p (a t) -> p a t", a=2)),
                            reads=["bank%d" % bt], writes=["QK%d.%d" % (qsl, T)])
                        P.op(DVE, lambda e: e.tensor_copy(out=Vb[qsl][:, T, :], in_=stage[ssl][:, 256:384]),
                             reads=[skey], writes=["Vb%d.%d" % (qsl, T)])

                    for T in range(17 + 2):
                        if T < 17:
                            stage_AB(T)
                        if T >= 2 and T - 2 < 16:
                            stage_C(T - 2)

                    def stage_S(T, g=g, qsl=qsl):
                        par = T % 2
                        kts = [T] + ([T - 1] if has_prev(g, T) else [])
                        nk = len(kts)
                        qkeys = ["QK%d.%d" % (qsl, t_) for t_ in kts]
                        for h in range(2):
                            P.mark(2.41)
                            bS = 4 + 2 * h + par

                            def mms(e, h=h, bS=bS):
                                for i_, kt in enumerate(kts):
                                    i = e.matmul(ps[:, bS, i_ * 128:(i_ + 1) * 128],
                                                 lhsT=QK[qsl][64 * h:64 * h + 64, 1, kt * 128:(kt + 1) * 128],
                                                 rhs=QK[qsl][64 * h:64 * h + 64, 0, T * 128:(T + 1) * 128],
                                                 start=True, stop=True)
                                return i
                            P.op(PE, mms, reads=qkeys, writes=["bank%d" % bS])
                            ptk = "pt%d%d" % (h, par)
                            P.op(ACT, lambda e, h=h, bS=bS: e.activation(
                                out=pt[h][par][:, 0:nk * 128], in_=ps[:, bS, 0:nk * 128], func=AF.Exp, scale=0.125),
                                reads=["bank%d" % bS], writes=[ptk])
                            P.mark(2.42)
                            P.op(DVE, lambda e, h=h: e.tensor_tensor(
                                out=pt[h][par][:, 0:nk * 128], in0=pt[h][par][:, 0:nk * 128],
                                in1=mask_b[:, 0:nk, :].rearrange("p a b -> p (a b)"), op=ALU.mult),
                                reads=[ptk], writes=[ptk])

                    def stage_O(T, g=g, qsl=qsl):
                        par = T % 2
                        kts = [T] + ([T - 1] if has_prev(g, T) else [])
                        nk_ = len(kts)
                        base, stride = tile_cols(g, T)
                        vkeys = ["Vb%d.%d" % (qsl, t_) for t_ in kts]
                        bO = 2 + par
                        P.mark(2.43)

                        def mmo(e):
                            for h in range(2):
                                for i_, kt in enumerate(kts):
                                    e.matmul(ps[64 * h:64 * h + 64, bO, 0:128], lhsT=Vb[qsl][:, kt, 64 * h:64 * h + 64],
                                             rhs=pt[h][par][:, i_ * 128:(i_ + 1) * 128], start=(i_ == 0), stop=(i_ == nk_ - 1))
                                for i_, kt in enumerate(kts):
                                    i = e.matmul(ps[64 * h:64 * h + 64, bO, 128:256], lhsT=ones_b[:, 0:64],
                                                 rhs=pt[h][par][:, i_ * 128:(i_ + 1) * 128], start=(i_ == 0), stop=(i_ == nk_ - 1))
                            return i
                        P.op(PE, mmo, reads=vkeys + ["pt0%d" % par, "pt1%d" % par], writes=["bank%d" % bO])
                        P.mark(2.44)
                        P.op(ACT, lambda e: e.copy(
                            out=OT[:, g, base:base + 127 * stride + 1:stride], in_=ps[:, bO, 0:128]),
                            reads=["bank%d" % bO], writes=["OT%d" % g])
                        if g == 0:
                            P.op(DVE, lambda e: e.tensor_copy(
                                out=Dn[:, base:base + 127 * stride + 1:stride], in_=ps[:, bO, 128:256]),
                                reads=["bank%d" % bO], writes=["Dn"])
                        else:
                            P.op(DVE, lambda e: e.tensor_tensor(
                                out=Dn[:, base:base + 127 * stride + 1:stride], in0=ps[:, bO, 128:256],
                                in1=Dn[:, base:base + 127 * stride + 1:stride], op=ALU.add),
                                reads=["bank%d" % bO, "Dn"], writes=["Dn"])

                    P.mark(2.4)
                    for T in range(17):
                        if T < 16:
                            stage_S(T)
                        if T >= 1:
                            stage_O(T - 1)
                P.mark(2.5)
                for k4 in range(4):
                    P.op(DVE, lambda e, k4=k4: e.reciprocal(out=rD[:, k4 * 512:(k4 + 1) * 512], in_=Dn[:, k4 * 512:(k4 + 1) * 512]),
                         reads=["Dn"], writes=["rD%d" % k4])
                ci = 0
                for g in range(3):
                    gi = 2 * g + sp
                    for (t0, n_) in TCH:
                        par = ci % 2
                        ci += 1
                        bG = par

                        def mmg(e, g=g, t0=t0, n_=n_, bG=bG):
                            for kc in range(8):
                                i = e.matmul(ps[:, bG, 0:n_], lhsT=wga[g][:, kc, :], rhs=hT[:, kc, t0:t0 + n_],
                                             start=(kc == 0), stop=(kc == 7))
                            return i
                        P.op(PE, mmg, reads=["hT", "W:wga%d" % g], writes=["bank%d" % bG])
                        if t0 == S:
                            P.op(ACT, lambda e, gi=gi, bG=bG: e.activation(out=sga_s[:, gi, :], in_=ps[:, bG, 0:NS], func=AF.Silu),
                                 reads=["bank%d" % bG], writes=["sga_s%d" % gi])
                            continue
                        P.op(ACT, lambda e, par=par, bG=bG: e.activation(out=sga[par][:, :], in_=ps[:, bG, 0:512], func=AF.Silu),
                             reads=["bank%d" % bG], writes=["sga%d" % par])
                        P.op(DVE, lambda e, g=g, t0=t0, par=par: e.tensor_tensor(
                            out=tt[par][:, :], in0=OT[:, g, t0:t0 + 512], in1=rD[:, t0:t0 + 512], op=ALU.mult),
                            reads=["OT%d" % g, "rD%d" % (t0 // 512)], writes=["tt"])
                        P.op(DVE, lambda e, gi=gi, t0=t0, par=par: e.tensor_tensor(
                            out=aT[:, gi, t0:t0 + 512], in0=tt[par][:, :], in1=sga[par][:, :], op=ALU.mult),
                            reads=["tt", "sga%d" % par], writes=["aT"])
            P.barrier()
            P.mark(3)
            for g in range(3):
                L = GROUPS[g][0]
                P.op(SP, lambda e, g=g, L=L: e.dma_start(
                    out=sk[g][l, :, L - 1, :].rearrange("b (s c) -> b s c", s=2), in_=qkvs[0:NS, 2 * g:2 * g + 2, 128:256]),
                    reads=["qkvs%d" % (2 * g), "qkvs%d" % (2 * g + 1)], dma="sknew")
                P.op(SP, lambda e, g=g, L=L: e.dma_start(
                    out=sv[g][l, :, L - 1, :].rearrange("b (s c) -> b s c", s=2), in_=qkvs[0:NS, 2 * g:2 * g + 2, 256:384]),
                    reads=["qkvs%d" % (2 * g), "qkvs%d" % (2 * g + 1)], dma="sknew")
            for part in range(3):
                P.op(POOL, lambda e, part=part: e.dma_start(
                    out=wconv[:, :, part * 768:(part + 1) * 768],
                    in_=w_in[l, :, 3072 + part * 768:3072 + (part + 1) * 768].rearrange("(k p) c -> p k c", p=128)),
                    writes=["W:wconv%d" % part], dma="wconv", nobar=True)
            qall = ["qkvs%d" % i_ for i_ in range(6)]
            P.op(ACT, lambda e: e.copy(out=qs_bf[0:NS, :, :], in_=qkvs[0:NS, :, 0:128]), reads=qall, writes=["qs_bf"])
            for b in range(NS):
                def mmb(e, b=b):
                    e.matmul(ps[:, 2, 0:512], lhsT=sel_b[:, b, :], rhs=qs_bf[0:NS, 0:4, :], start=True, stop=True)
                    return e.matmul(ps[:, 3, 0:256], lhsT=sel_b[:, b, :], rhs=qs_bf[0:NS, 4:6, :], start=True, stop=True)
                P.op(PE, mmb, reads=["qs_bf"], writes=["bank2", "bank3"])
                P.op(ACT, lambda e, b=b: e.copy(out=qb[:, b, :], in_=ps[:, 2:4, :].rearrange("p a b -> p (a b)")[:, 0:768]),
                     reads=["bank2", "bank3"], writes=["qb%d" % b])
            qbk = ["qb%d" % b for b in range(NS)]
            P.op(DVE, lambda e: e.tensor_tensor(out=qb[:, :, :], in0=Kc[:, :, :], in1=qb[:, :, :], op=ALU.mult),
                 reads=qbk + ["W:Kc0", "W:Kc1", "W:Kc2"], writes=["prod"])
            Ssc = p3_small[:, 0:48]
            P.op(DVE, lambda e: e.tensor_reduce(out=Ssc, in_=qb[:, :, :].rearrange("p b (h d) -> p (b h) d", d=64),
                                                axis=AX.X, op=ALU.add), reads=["prod"], writes=["Ssc"])
            Pbb = Pb[:, :, :].rearrange("p b c -> p (b c)")[:, 0:1536].bitcast(BF16).rearrange("p (b c) -> p b c", b=NS)
            P.op(ACT, lambda e: e.activation(out=Pbb.rearrange("p b (h d) -> p (b h) d", d=64),
                                             in_=Ssc.rearrange("p (a o) -> p a o", o=1).to_broadcast([128, 48, 64]),
                                             func=AF.Exp, scale=0.125), reads=["Ssc"], writes=["Pb"])
            P.op(DVE, lambda e: e.tensor_tensor(out=PVb[:, :, :], in0=Vc[:, :, :], in1=Pbb, op=ALU.mult),
                 reads=["Pb", "W:Vc0", "W:Vc1", "W:Vc2"], writes=["PV"])
            sn = p3_small[0:NS, 48:60]
            pn3 = pnx[0:NS, 0, :].rearrange("p (a c) -> p a c", a=6)
            pbn = pnx[0:NS, 1, :].bitcast(BF16)[:, 0:768].rearrange("p (a c) -> p a c", a=6)
            pvn = pnx[0:NS, 2, :].bitcast(BF16)[:, 0:768].rearrange("p (a c) -> p a c", a=6)
            P.op(DVE, lambda e: e.tensor_tensor(out=pn3, in0=qkvs[0:NS, :, 0:128], in1=qkvs[0:NS, :, 128:256], op=ALU.mult),
                 reads=qall, writes=["pn3"])
            P.op(DVE, lambda e: e.tensor_reduce(out=sn, in_=pn3.rearrange("p a (h d) -> p (a h) d", d=64), axis=AX.X, op=ALU.add),
                 reads=["pn3"], writes=["sn"])
            P.op(ACT, lambda e: e.activation(out=pbn.rearrange("p a (h d) -> p (a h) d", d=64),
                                             in_=sn.rearrange("p (a o) -> p a o", o=1).to_broadcast([NS, 12, 64]),
                                             func=AF.Exp, scale=0.125), reads=["sn"], writes=["pbn"])
            P.op(DVE, lambda e: e.tensor_tensor(out=pvn, in0=qkvs[0:NS, :, 256:384], in1=pbn, op=ALU.mult),
                 reads=qall + ["pbn"], writes=["pvn"])

            def mmsa(e):
                for c in range(6):
                    e.matmul(ps[:, 4, c * 4:c * 4 + 4], lhsT=pvn[:, c, :], rhs=ident_b[0:NS, 0:NS], start=True, stop=False,
                             skip_group_check=True)
                    for b in range(NS):
                        e.matmul(ps[:, 4, c * 4 + b:c * 4 + b + 1], lhsT=PVb[:, b, c * 128:(c + 1) * 128], rhs=ones_b[:, 0:1],
                                 start=False, stop=(b == NS - 1), skip_group_check=True)
                for c in range(6):
                    e.matmul(ps[:, 5, c * 4:c * 4 + 4], lhsT=pbn[:, c, :], rhs=ident_b[0:NS, 0:NS], start=True, stop=False,
                             skip_group_check=True)
                    for b in range(NS):
                        i = e.matmul(ps[:, 5, c * 4 + b:c * 4 + b + 1], lhsT=Pbb[:, b, c * 128:(c + 1) * 128], rhs=ones_b[:, 0:1],
                                     start=False, stop=(b == NS - 1), skip_group_check=True)
                return i
            P.op(PE, mmsa, reads=["PV", "Pb", "pvn", "pbn"], writes=["bank4", "bank5"])
            dn = p3_small[:, 64:88]
            Ds = p3_small[:, 96:104]
            as1 = p3_small[:, 128:152]
            P.op(ACT, lambda e: e.copy(out=dn, in_=ps[:, 5, 0:24]), reads=["bank5"], writes=["dn"])
            P.op(DVE, lambda e: e.tensor_tensor(out=Ds, in0=dn[:, 0:8], in1=dn[:, 8:16], op=ALU.add), reads=["dn"], writes=["Ds"])
            P.op(DVE, lambda e: e.tensor_tensor(out=Ds, in0=Ds, in1=dn[:, 16:24], op=ALU.add), reads=["dn", "Ds"], writes=["Ds"])
            P.op(DVE, lambda e: e.reciprocal(out=Ds, in_=Ds), reads=["Ds"], writes=["Ds"])
            P.op(DVE, lambda e: e.tensor_tensor(out=as1.rearrange("p (g x) -> p g x", g=3),
                                                in0=ps[:, 4, 0:24].rearrange("p (g x) -> p g x", g=3),
                                                in1=Ds.rearrange("p (o x) -> p o x", o=1).to_broadcast([128, 3, 8]), op=ALU.mult),
                 reads=["bank4", "Ds"], writes=["as1"])
            P.op(DVE, lambda e: e.tensor_tensor(out=aT[:, :, S:S + NS], in0=as1.rearrange("p (c b) -> p c b", c=6),
                                                in1=sga_s[:, :, :], op=ALU.mult),
                 reads=["as1"] + ["sga_s%d" % i_ for i_ in range(6)], writes=["aT"])
            P.barrier()
            P.mark(4)
            c4 = Carver(CT_B)
            diag = [c4.take([31, 128], BF16) for _ in range(2)]
            cdw_f = c4.take([6, 512], F32)
            u_bf = c4.take([6, 542], BF16)
            assert c4.off <= WCONV_OFF, c4.off
            c4 = Carver(WCONV_OFF + 36864)
            cdw_b = c4.take([6, 512], BF16)
            cdw2_b = c4.take([6, 512], BF16)
            sgc = c4.take([6, 512], BF16)
            sgt = [c4.take([512], F32) for _ in range(2)]
            mean = c4.take([512], F32)
            vv = c4.take([512], F32)
            ybf = [c4.take([512], BF16) for _ in range(2)]
            u_last = c4.take([6, 30], F32)
            cw_tok = c4.take([768], F32)
            cwT = c4.take([6, 31], F32)
            wck = ["W:wconv0", "W:wconv1", "W:wconv2"]
            sgl = [cw_tok[:, 0:256].bitcast(BF16), cw_tok[:, 256:512].bitcast(BF16)]
            cdws0 = small[:, 16:40]
            st = cdw_f[0:30, :, :].rearrange("p a b -> p (a b)").rearrange("p (b c) -> p b c", b=NS)

            P.op(SP, lambda e: e.dma_start(out=cw_tok[0:31, :], in_=conv_w[l]), writes=["cw_tok"], dma="cw")
            P.op(SP, lambda e: e.dma_start(out=st, in_=stc[l].rearrange("b j c -> j b c")), writes=["cdw_f"], dma="st")
            emit_copies(l, gate=wck)

            P.op(DVE, lambda e: e.memset(u_bf[:, :, 0:30], 0.0), writes=["u%d" % c for c in range(6)])
            def p4_setup():
                def trcw(e):
                    for c in range(6):
                        i = e.transpose(out=ps[:, 7, c * 32:c * 32 + 31], in_=cw_tok[0:31, c * 128:(c + 1) * 128],
                                        identity=ident_f[0:31, 0:31])
                    return i
                P.op(PE, trcw, reads=["cw_tok"], writes=["bank7"])
                P.op(ACT, lambda e: e.copy(out=cwT[:, :, :], in_=ps[:, 7, 0:192].rearrange("p (c j) -> p c j", c=6)[:, :, 0:31]),
                     reads=["bank7"], writes=["cwT"])
                P.op(DVE, lambda e: e.tensor_tensor(out=st, in0=st,
                                                    in1=cw_tok[0:30, :].rearrange("p (o c) -> p o c", o=1).to_broadcast([30, NS, 768]),
                                                    op=ALU.mult), reads=["cdw_f", "cw_tok"], writes=["cdw_f"])

                def mmsc(e):
                    for c in range(6):
                        for b in range(NS):
                            i = e.matmul(ps[:, 6, c * 4 + b:c * 4 + b + 1], lhsT=st[:, b, c * 128:(c + 1) * 128], rhs=ones_f[0:30, 0:1],
                                         start=True, stop=True, skip_group_check=True)
                    return i
                P.op(PE, mmsc, reads=["cdw_f"], writes=["bank6"])
                P.op(ACT, lambda e: e.copy(out=cdws0, in_=ps[:, 6, 0:24]), reads=["bank6"], writes=["cdws0"])


            def main_ctx(t0):
                return dict(t0=t0, k=t0 // 512, n=512, samp=False, pfx="", cdw_f=cdw_f, cdw_b=cdw_b, cdw2_b=cdw2_b, sgc=sgc, sgt=sgt,
                            mean=mean, vv=vv, ybf=ybf, sgl=sgl)
            s_ctx = dict(t0=S, k=4, n=NS, samp=True, pfx="s_", cdw_f=s_cdwf, cdw_b=s_cdwb, cdw2_b=s_cdw2b, sgc=s_sgc, sgt=s_sgt,
                         mean=s_mean, vv=s_vv, ybf=s_ybf, sgl=s_sgl)

            def projAB(X, c):
                t0, n_, px = X["t0"], X["n"], X["pfx"]
                sp_ = c % 2
                sg = X["sgt"][sp_]
                if X["samp"]:
                    pa = ps[:, 7, 200 + c * 8:200 + c * 8 + 4]
                    pb = ps[:, 7, 204 + c * 8:204 + c * 8 + 4]
                    bka = bkb = "bank7"
                else:
                    ba = 2 * (c % 2)
                    pa = ps[:, ba, 0:512]
                    pb = ps[:, ba + 1, 0:512]
                    bka, bkb = "bank%d" % ba, "bank%d" % (ba + 1)

                def mmp(e):
                    for dst, off in ((pa, 0), (pb, 768)):
                        for kc in range(8):
                            i = e.matmul(dst, lhsT=wconv[:, kc, off + c * 128:off + (c + 1) * 128],
                                         rhs=hT[:, kc, t0:t0 + n_], start=(kc == 0), stop=(kc == 7), skip_group_check=True)
                    return i
                P.op(PE, mmp, reads=["hT"] + wck, writes=list({bka, bkb}))
                P.op(ACT, lambda e: e.activation(out=sg[:, 0:n_], in_=pb, func=AF.Sigmoid),
                     reads=[bkb], writes=[px + "sgt%d" % sp_])
                if X["samp"]:
                    P.op(DVE, lambda e: e.tensor_tensor(out=u_s[:, c, :], in0=pa, in1=sg[:, 0:NS], op=ALU.mult),
                         reads=[bka, px + "sgt%d" % sp_], writes=["u_s%d" % c])
                    P.op(DVE, lambda e: e.scalar_tensor_tensor(
                        out=s_cdwf[:, c, :], in0=u_s[:, c, :], scalar=cwT[:, c, 30:31], in1=cdws0[:, c * 4:(c + 1) * 4],
                        op0=ALU.mult, op1=ALU.add),
                        reads=["u_s%d" % c, "cwT", "cdws0"], writes=["s_cdwf%d" % c])
                    P.op(DVE, lambda e: e.tensor_copy(out=s_cdwb[:, c, :], in_=s_cdwf[:, c, :]),
                         reads=["s_cdwf%d" % c], writes=["s_cdwb%d" % c])
                    P.op(ACT, lambda e: e.activation(out=s_cdw2b[:, c, :], in_=s_cdwf[:, c, :], func=AF.Square),
                         reads=["s_cdwf%d" % c], writes=["s_cdw2b%d" % c])
                    return
                P.op(DVE, lambda e: e.tensor_tensor(out=u_bf[:, c, 30:542], in0=pa, in1=sg[:, :], op=ALU.mult),
                     reads=[bka, "sgt%d" % sp_], writes=["u%d" % c])
                if t0 == 1536:
                    P.op(DVE, lambda e: e.tensor_tensor(out=u_last[:, c, :], in0=ps[:, 2 * (c % 2), 482:512],
                                                        in1=sg[:, 482:512], op=ALU.mult),
                         reads=[bka, "sgt%d" % sp_], writes=["u_last%d" % c])
                dsl = c % 2
                if t0 == 0 and c == 0:
                    p4_setup()
                P.op(DVE, lambda e: e.tensor_tensor(
                    out=diag[dsl][:, :, :], in0=ident_b[:, :].rearrange("p (o q) -> p o q", o=1).to_broadcast([128, 31, 128]),
                    in1=cwT[:, c, :].rearrange("p (j o) -> p j o", o=1).to_broadcast([128, 31, 128]), op=ALU.mult),
                    reads=["cwT", "ident_b"], writes=["diag%d" % dsl])

            def convc(X, c):
                dsl = c % 2
                bD = 4 + dsl
                first = (X["t0"] == 0)

                def mmc(e):
                    for j in range(31):
                        i = e.matmul(ps[:, bD, :], lhsT=diag[dsl][:, j, :], rhs=u_bf[:, c, j:j + 512],
                                     start=(j == 0), stop=(j == 30))
                    return i
                P.op(PE, mmc, reads=["diag%d" % dsl, "u%d" % c], writes=["bank%d" % bD])
                P.op(ACT, lambda e: e.copy(out=cdw_f[:, c, :], in_=ps[:, bD, :]),
                     reads=["bank%d" % bD], writes=["cdwf%d" % c] + (["cdw_f"] if first else []))
                P.op(DVE, lambda e: e.tensor_copy(out=cdw_b[:, c, :], in_=ps[:, bD, :]),
                     reads=["bank%d" % bD], writes=["cdwb%d" % c])
                P.op(ACT, lambda e: e.activation(out=cdw2_b[:, c, :], in_=ps[:, bD, :], func=AF.Square),
                     reads=["bank%d" % bD], writes=["cdw2b%d" % c])
                P.op(DVE, lambda e: e.tensor_copy(out=u_bf[:, c, 0:30], in_=u_bf[:, c, 512:542]),
                     reads=["u%d" % c, "cdwb%d" % c], writes=["u%d" % c])

            def projG(X, c):
                t0, n_, px = X["t0"], X["n"], X["pfx"]
                sp_ = c % 2
                sg = X["sgt"][sp_]
                if X["samp"]:
                    pg = ps[:, 7, 256 + c * 4:256 + c * 4 + 4]
                    bkg = "bank7"
                else:
                    pg = ps[:, c % 4, 0:512]
                    bkg = "bank%d" % (c % 4)

                def mmg4(e):
                    for kc in range(8):
                        i = e.matmul(pg, lhsT=wconv[:, kc, 1536 + c * 128:1536 + (c + 1) * 128],
                                     rhs=hT[:, kc, t0:t0 + n_], start=(kc == 0), stop=(kc == 7), skip_group_check=True)
                    return i
                P.op(PE, mmg4, reads=["hT"] + wck, writes=[bkg])
                P.op(ACT, lambda e: e.activation(out=sg[:, 0:n_], in_=pg, func=AF.Sigmoid),
                     reads=[bkg], writes=[px + "sgt%d" % sp_])
                P.op(DVE, lambda e: e.tensor_tensor(out=X["sgc"][:, c, 0:n_], in0=pg, in1=sg[:, 0:n_], op=ALU.mult),
                     reads=[bkg, px + "sgt%d" % sp_], writes=[px + "sgc%d" % c])

            def ln_head(X):
                n_, px = X["n"], X["pfx"]
                mean_, vv_ = X["mean"], X["vv"]
                if X["samp"]:
                    p5 = ps[:, 7, 300:304]
                    p6 = ps[:, 7, 304:308]
                    bk5 = bk6 = "bank7"
                else:
                    p5 = ps[:, 5, 0:512]
                    p6 = ps[:, 6, 0:512]
                    bk5, bk6 = "bank5", "bank6"

                def mmst(e):
                    for c in range(6):
                        e.matmul(p5, lhsT=ones_b[:, :], rhs=X["cdw_b"][:, c, 0:n_], start=(c == 0), stop=(c == 5), skip_group_check=True)
                    for c in range(6):
                        i = e.matmul(p6, lhsT=ones_b[:, :], rhs=X["cdw2_b"][:, c, 0:n_], start=(c == 0), stop=(c == 5), skip_group_check=True)
                    return i
                P.op(PE, mmst, reads=[px + "cdwb%d" % c for c in range(6)] + [px + "cdw2b%d" % c for c in range(6)], writes=list({bk5, bk6}))
                P.op(ACT, lambda e: e.mul(out=mean_[:, 0:n_], in_=p5, mul=1.0 / 768), reads=[bk5], writes=[px + "mean"])
                P.op(DVE, lambda e: e.tensor_tensor(out=vv_[:, 0:n_], in0=mean_[:, 0:n_], in1=mean_[:, 0:n_], op=ALU.mult),
                     reads=[px + "mean"], writes=[px + "vv"])
                P.op(DVE, lambda e: e.scalar_tensor_tensor(out=vv_[:, 0:n_], in0=p6, scalar=1.0 / 768, in1=vv_[:, 0:n_],
                                                           op0=ALU.mult, op1=ALU.subtract),
                     reads=[bk6, px + "vv"], writes=[px + "vv"])
                P.op(ACT, lambda e: e.activation(out=vv_[:, 0:n_], in_=vv_[:, 0:n_], func=AF.Sqrt, bias=eps_ln[:, :]),
                     reads=[px + "vv"], writes=[px + "vv"])
                P.op(DVE, lambda e: e.reciprocal(out=vv_[:, 0:n_], in_=vv_[:, 0:n_]), reads=[px + "vv"], writes=[px + "vv"])

            def ln_apply(X, c):
                t0, n_, px = X["t0"], X["n"], X["pfx"]
                ysl = c % 2
                cf = X["cdw_f"][:, c, 0:n_]
                yb = X["ybf"][ysl][:, 0:n_]
                sl_ = X["sgl"][ysl][:, 0:n_]
                P.op(DVE, lambda e: e.tensor_tensor(out=cf, in0=cf, in1=X["mean"][:, 0:n_], op=ALU.subtract),
                     reads=[px + "cdwf%d" % c, px + "mean"], writes=[px + "cdwf%d" % c])
                P.op(DVE, lambda e: e.tensor_tensor(out=cf, in0=cf, in1=X["vv"][:, 0:n_], op=ALU.mult),
                     reads=[px + "cdwf%d" % c, px + "vv"], writes=[px + "cdwf%d" % c])
                P.op(ACT, lambda e: e.activation(out=yb, in_=cf, func=AF.Identity, scale=lngT[:, l, c:c + 1], bias=lnbT[:, l, c:c + 1]),
                     reads=[px + "cdwf%d" % c], writes=[px + "ybf%d" % ysl])
                P.op(ACT, lambda e: e.activation(out=sl_, in_=cf, func=AF.Sigmoid, scale=lngT[:, l, c:c + 1], bias=lnbT[:, l, c:c + 1]),
                     reads=[px + "cdwf%d" % c, "cwT"], writes=[px + "sgl%d" % ysl] + ([] if X["samp"] else ["cw_tok"]))
                P.op(DVE, lambda e: e.tensor_tensor(out=yb, in0=yb, in1=sl_, op=ALU.mult),
                     reads=[px + "ybf%d" % ysl, px + "sgl%d" % ysl], writes=[px + "ybf%d" % ysl])
                P.op(DVE, lambda e: e.tensor_tensor(out=cT[:, c, t0:t0 + n_], in0=yb, in1=X["sgc"][:, c, 0:n_], op=ALU.mult),
                     reads=[px + "ybf%d" % ysl, px + "sgc%d" % c], writes=["cT.%d" % X["k"], px + "cdwf%d" % c])

            c5 = Carver(CT_B + MT_B)
            wm = [c5.take([28, 128], BF16) for _ in range(2)]
            sm = [[c5.take([512], F32) for _ in range(2)] for _ in range(2)]
            m1 = [[c5.take([512], F32) for _ in range(2)] for _ in range(2)]
            p5ci = [0]

            def p5_load(mo):
                wsl = mo % 2
                if mo == 0:
                    P.op(POOL, lambda e: e.memset(small[:, 63:64], 0.0), writes=["wmfence"] + ["u%d" % c_ for c_ in range(6)] + wck)
                rd = ["wmfence"] if mo <= 1 else []
                P.op(POOL, lambda e: e.dma_start(
                    out=wm[wsl][:, 0:6, :], in_=w_ao[l, :, mo * 128:(mo + 1) * 128].rearrange("(k p) c -> p k c", p=128)),
                    reads=rd, writes=["wm%d.0" % wsl], dma="wm%d" % wsl)
                P.op(POOL, lambda e: e.dma_start(
                    out=wm[wsl][:, 6:12, :], in_=w_co[l, :, mo * 128:(mo + 1) * 128].rearrange("(k p) c -> p k c", p=128)),
                    reads=rd, writes=["wm%d.1" % wsl], dma="wm%d" % wsl)
                P.op(POOL, lambda e: e.dma_start(
                    out=wm[wsl][:, 12:20, :], in_=w_in[l, :, 5376 + mo * 128:5376 + (mo + 1) * 128].rearrange("(k p) c -> p k c", p=128)),
                    reads=rd, writes=["wm%d.2" % wsl], dma="wm%d" % wsl)
                P.op(POOL, lambda e: e.dma_start(
                    out=wm[wsl][:, 20:28, :], in_=w_in[l, :, 6400 + mo * 128:6400 + (mo + 1) * 128].rearrange("(k p) c -> p k c", p=128)),
                    reads=rd, writes=["wm%d.3" % wsl], dma="wm%d" % wsl)

            def p5_chunk(mo, k5):
                wsl = mo % 2
                t0, n_ = TCH[k5]
                wmk = ["wm%d.%d" % (wsl, i_) for i_ in range(4)]
                par = p5ci[0] % 2
                p5ci[0] += 1
                b0 = 4 * par

                def mm5(e):
                    for kc in range(6):
                        e.matmul(ps[:, b0, 0:n_], lhsT=wm[wsl][:, kc, :], rhs=aT[:, kc, t0:t0 + n_], start=(kc == 0), stop=(kc == 5))
                    for kc in range(6):
                        e.matmul(ps[:, b0 + 1, 0:n_], lhsT=wm[wsl][:, 6 + kc, :], rhs=cT[:, kc, t0:t0 + n_], start=(kc == 0), stop=(kc == 5))
                    for kc in range(8):
                        e.matmul(ps[:, b0 + 2, 0:n_], lhsT=wm[wsl][:, 12 + kc, :], rhs=hT[:, kc, t0:t0 + n_], start=(kc == 0), stop=(kc == 7))
                    for kc in range(8):
                        i = e.matmul(ps[:, b0 + 3, 0:n_], lhsT=wm[wsl][:, 20 + kc, :], rhs=hT[:, kc, t0:t0 + n_], start=(kc == 0), stop=(kc == 7))
                    return i
                bk = ["bank%d" % (b0 + i_) for i_ in range(4)]
                P.op(PE, mm5, reads=["hT", "aT", "cT.%d" % k5] + wmk, writes=bk)
                for i_ in range(2):
                    P.op(ACT, lambda e, i_=i_: e.activation(
                        out=sm[par][i_][:, 0:n_], in_=ps[:, b0 + 2 + i_, 0:n_], func=AF.Sigmoid),
                        reads=[bk[2 + i_]], writes=["sm%d%d" % (par, i_)])
                    P.op(DVE, lambda e, i_=i_: e.tensor_tensor(
                        out=m1[par][i_][:, 0:n_], in0=ps[:, b0 + i_, 0:n_], in1=sm[par][i_][:, 0:n_], op=ALU.mult),
                        reads=[bk[i_], "sm%d%d" % (par, i_)], writes=["m1%d%d" % (par, i_)])
                P.op(DVE, lambda e: e.tensor_tensor(
                    out=mT[:, mo, t0:t0 + n_], in0=m1[par][0][:, 0:n_], in1=m1[par][1][:, 0:n_], op=ALU.add),
                    reads=["m1%d0" % par, "m1%d1" % par], writes=["mT"] + (["diag0"] if mo == 0 else []))

            ctxs = [main_ctx(t0_) for (t0_, _n) in TCH[0:4]]
            prev = None
            for ci4, X in enumerate(ctxs):
                lastmain = (ci4 == 3)
                for c in range(6 + 1):
                    if c < 6:
                        projAB(X, c)
                    if c < 6 and prev is not None:
                        ln_apply(prev, c)
                    if c < 6 and lastmain:
                        projAB(s_ctx, c)
                    if c >= 1:
                        convc(X, c - 1)
                for c in range(6):
                    projG(X, c)
                    if lastmain:
                        projG(s_ctx, c)
                ln_head(X)
                prev = X
            def trul(e):
                for c in range(6):
                    i = e.transpose(out=ps[0:30, 0 + c // 4, (c % 4) * 128:(c % 4 + 1) * 128], in_=u_last[:, c, :], identity=ident_f[:, :])
                return i
            P.op(PE, trul, reads=["u_last%d" % c for c in range(6)], writes=["bank0", "bank1"])
            P.op(ACT, lambda e: e.copy(out=us_tok[0:30, :], in_=ps[0:30, 0:2, :].rearrange("p a b -> p (a b)")[:, 0:768]),
                 reads=["bank0", "bank1"], writes=["ustok"])
            P.op(SP, lambda e: e.dma_start(out=pconv[l], in_=us_tok[0:30, :]), reads=["ustok"], dma="pconv")
            p5_load(0)
            p5_load(1)
            ln_head(s_ctx)
            for c in range(6):
                ln_apply(prev, c)
                ln_apply(s_ctx, c)
                if c < 3:
                    p5_chunk(0, c)

            def trus(e):
                for c in range(6):
                    i = e.transpose(out=ps[0:NS, 2 + c // 4, (c % 4) * 128:(c % 4 + 1) * 128], in_=u_s[:, c, :], identity=ident_f[:, :])
                return i
            P.op(PE, trus, reads=["u_s%d" % c for c in range(6)], writes=["bank2", "bank3"])
            P.op(ACT, lambda e: e.copy(out=us_tok[0:NS, :], in_=ps[0:NS, 2:4, :].rearrange("p a b -> p (a b)")[:, 0:768]),
                 reads=["bank2", "bank3"], writes=["ustok"])
            P.op(SP, lambda e: e.dma_start(out=sconv[l, :, 29, :], in_=us_tok[0:NS, :]), reads=["ustok"], dma="usout")
            p5_chunk(0, 3)
            p5_chunk(0, 4)
            P.mark(5)
            P.barrier()
            xsrc = x_p if l == 0 else y_p
            for half in range(2):
                P.op(POOL, lambda e, half=half: e.dma_start(
                    out=wo_sb[:, :, half * 512:(half + 1) * 512],
                    in_=w_o[l, :, half * 512:(half + 1) * 512].rearrange("(k p) c -> p k c", p=128)),
                    writes=["W:wo%d" % half], dma="wo", nobar=True)
            P.op(SP, lambda e: e.dma_start(out=gpost[:], in_=n_post[l].partition_broadcast(128)), writes=["W:gpost"], dma="W:gpost", nobar=True)
            if not last:
                P.op(SP, lambda e: e.dma_start(out=gpre[:], in_=n_pre[l + 1].partition_broadcast(128)), writes=["W:gpre"], dma="W:gpre", nobar=True)
            for T in range(3):
                P.op(SP, lambda e, T=T: e.dma_start(out=xt[T][:], in_=xsrc[T * 128:(T + 1) * 128, :]),
                     writes=["W:xt%d" % T], dma="W:xt%d" % T, nobar=True)
            for mo in range(1, 8):
                if mo >= 2:
                    p5_load(mo)
                for k5 in range(5):
                    p5_chunk(mo, k5)
            P.barrier()
            P.mark(6)
            if not last:
                prefetch_qkv(l + 1)
            xsrc = x_p if l == 0 else y_p
            PAIRS = ((0, 1), (2, 3), (6, 7))

            def p6_info(T):
                samp = (T == 16)
                npart = NS if samp else 128
                if samp:
                    return samp, npart, xs_t[:, :], "xs_t", S
                return samp, npart, xt[T % 3][:, :], "W:xt%d" % (T % 3), T * 128

            def p6_M(T):
                samp, npart, xcur, xkey, col0 = p6_info(T)
                if not samp and T >= 3:
                    sl = T % 3
                    P.op(SP, lambda e: e.dma_start(out=xt[sl][:], in_=xsrc[T * 128:(T + 1) * 128, :]),
                         reads=["ydram%d" % T], writes=[xkey], dma="W:xt%d" % sl)
                b0, b1 = PAIRS[T % 3]

                def mm6(e):
                    for half, bb in enumerate((b0, b1)):
                        for kc in range(8):
                            i = e.matmul(ps[0:npart, bb, :], lhsT=mT[:, kc, col0:col0 + npart],
                                         rhs=wo_sb[:, kc, half * 512:(half + 1) * 512], start=(kc == 0), stop=(kc == 7))
                    return i
                P.op(PE, mm6, reads=["mT", "W:wo0", "W:wo1"], writes=["bank%d" % b0, "bank%d" % b1])

            def p6_E(T):
                samp, npart, xcur, xkey, col0 = p6_info(T)
                b0, b1 = PAIRS[T % 3]
                par = T % 2
                ob = ps[0:npart, b0:b1 + 1, :].rearrange("p a b -> p (a b)")
                q3 = T % 3
                ss = small[0:npart, 40 + q3 * 4:40 + q3 * 4 + 1]
                sd = small[0:npart, 40 + q3 * 4 + 1:40 + q3 * 4 + 2]
                rs = small[0:npart, 40 + q3 * 4 + 2:40 + q3 * 4 + 3]
                kq = "q%d" % q3
                bks = ["bank%d" % b0, "bank%d" % b1]
                P.op(ACT, lambda e: e.activation(out=p1_junk[0:npart, :], in_=ob, func=AF.Square, accum_out=ss),
                     reads=bks, writes=["p1junk", kq + "ss"])
                P.op(ACT, lambda e: e.activation(out=sd, in_=ss, func=AF.Sqrt, scale=1.0 / DM, bias=eps_rms[0:npart, :]),
                     reads=[kq + "ss"], writes=[kq + "sd"])
                P.op(DVE, lambda e: e.reciprocal(out=rs, in_=sd), reads=[kq + "sd"], writes=[kq + "rs"])
                P.op(DVE, lambda e: e.tensor_tensor(out=p6_t[par][0:npart, :], in0=ob, in1=gpost[0:npart, :], op=ALU.mult),
                     reads=bks + ["W:gpost"], writes=["p6t"])
                P.op(DVE, lambda e: e.scalar_tensor_tensor(
                    out=xcur, in0=p6_t[par][0:npart, :], scalar=rs, in1=xcur, op0=ALU.mult, op1=ALU.add),
                    reads=["p6t", kq + "rs", xkey], writes=[xkey])
                if not samp:
                    sl = T % 3
                    P.op(SP, lambda e: e.dma_start(out=y_p[T * 128:(T + 1) * 128, :], in_=xt[sl][:]),
                         reads=[xkey], writes=["ydram%d" % T], dma="xo%d" % sl)
                elif last:
                    P.op(SP, lambda e: e.dma_start(out=y_s, in_=xs_t[:, :]), reads=["xs_t"], dma="ys")
                if not last:
                    p1_norm(xcur, npart, xkey, T % 3)

            def p6_R(T):
                samp, npart, xcur, xkey, col0 = p6_info(T)
                if not last:
                    p1_tr(npart, col0, T % 3, T % 2)

            for T in range(17 + 2):
                if T < 17:
                    p6_M(T)
                if T >= 2:
                    p6_R(T - 2)
                if T < 17:
                    p6_E(T)
            P.barrier()

        for l_ in range(nlay):
            layer(l_)
        P.build()
        stats = P.stats
    return nc, stats


_CACHE = {}


def _consts():
    ident = np.eye(128, dtype=np.float32)
    k = np.arange(128)[:, None]
    q = np.arange(128)[None, :]
    mask = np.stack([(k <= q), (k >= q)], axis=1).astype(np.float32)
    inv = np.power(np.float32(500000.0), -np.arange(0, 16, 2, dtype=np.float32) / np.float32(16)).astype(np.float32)
    rope = np.zeros((128, 2, 48, 8), np.float32)
    p = np.arange(128)
    for g in range(3):
        for T in range(16):
            base, stride = tile_cols(g, T)
            pos = (base + stride * p).astype(np.float32)
            ang = (pos[:, None] * inv[None, :]).astype(np.float32)
            rope[:, 0, g * 16 + T, :] = np.cos(ang)
            rope[:, 1, g * 16 + T, :] = np.sin(ang)
    angs = (np.float32(8192.0) * inv).astype(np.float32)
    ropes = np.zeros((NS, 2, 8), np.float32)
    ropes[:, 0, :] = np.cos(angs)[None, :]
    ropes[:, 1, :] = np.sin(angs)[None, :]
    sel = np.zeros((NS, NS, 128), np.float32)
    for b in range(NS):
        sel[b, b, :] = 1.0
    return dict(c_ident=ident, c_mask=mask, c_rope=rope, c_ropes=ropes, c_sel=sel)


def kernel(x_prompt, x_sample, cache_k0, cache_v0, cache_k1, cache_v1, cache_k2, cache_v2,
           state_conv, w_in, w_attn_out, w_conv_out, w_out, conv_w, conv_ln_g, conv_ln_b,
           norm_pre, norm_post):
    ncores = 8
    if "nc" not in _CACHE:
        _CACHE["nc"] = build_program(NL)[0]
    nc = _CACHE["nc"]
    f = lambda a: np.ascontiguousarray(np.asarray(a, dtype=np.float32))
    consts = _consts()
    caches_k = [f(cache_k0), f(cache_k1), f(cache_k2)]
    caches_v = [f(cache_v0), f(cache_v1), f(cache_v2)]
    shared = dict(w_in=f(w_in), w_ao=f(w_attn_out), w_co=f(w_conv_out), w_o=f(w_out), conv_w=f(conv_w),
                  ln_g=f(conv_ln_g), ln_b=f(conv_ln_b), n_pre=f(norm_pre), n_post=f(norm_post), **consts)
    xp = f(x_prompt)
    xs = f(x_sample)
    sc = f(state_conv)
    in_maps = []
    for c in range(ncores):
        m = dict(shared)
        m["x_p"] = xp[c]
        m["x_s"] = np.ascontiguousarray(xs[NS * c:NS * (c + 1), 0, :])
        for g in range(3):
            W = GROUPS[g][0]
            m["ck%d" % g] = np.ascontiguousarray(caches_k[g][:, NS * c:NS * (c + 1)].reshape(NL, NS, W, 256))
            m["cv%d" % g] = np.ascontiguousarray(caches_v[g][:, NS * c:NS * (c + 1)].reshape(NL, NS, W, 256))
        m["stc"] = np.ascontiguousarray(sc[:, NS * c:NS * (c + 1)])
        in_maps.append(m)
    res = run_bass_kernel_spmd(nc, in_maps, core_ids=list(range(ncores)))
    R = res.results
    yp = np.stack([R[c]["y_p"] for c in range(ncores)], axis=0)
    ys = np.concatenate([R[c]["y_s"] for c in range(ncores)], axis=0)[:, None, :]
    outs = [yp, ys]
    for g in range(3):
        W = GROUPS[g][0]
        for nm in ("pk", "pv"):
            outs.append(np.stack([R[c]["%s%d" % (nm, g)] for c in range(ncores)], axis=1).reshape(NL, ncores, W, 4, 64))
    outs.append(np.stack([R[c]["pconv"] for c in range(ncores)], axis=1))
    for g in range(3):
        W = GROUPS[g][0]
        for nm in ("sk", "sv"):
            outs.append(np.concatenate([R[c]["%s%d" % (nm, g)] for c in range(ncores)], axis=1).reshape(NL, NS * ncores, W, 4, 64))
    outs.append(np.concatenate([R[c]["sconv"] for c in range(ncores)], axis=1))
    return tuple(np.ascontiguousarray(o, dtype=np.float32) for o in outs)
```
